# Optimizing a Trainium2 kernel written in Bass

```python
import math
import jax, jax.numpy as jnp
from jax import lax
import numpy as np

D_MODEL = 1024
BATCH = 8
SEQ = 4096
DEPTH = 2

CHUNK = 64
EPS = 1e-6
D_MLSTM = D_MODEL // 2
D_RGLRU = D_MODEL // 2
D_SSD = D_MODEL // 2
D_MIX = D_MLSTM + D_RGLRU + D_SSD
MLSTM_HEADS = 4
MLSTM_HEAD_DIM = D_MLSTM // MLSTM_HEADS
RGLRU_BLOCKS = 8
RGLRU_BLOCK_DIM = D_RGLRU // RGLRU_BLOCKS
RGLRU_C = 8.0
SSD_HEAD_DIM = 64
SSD_HEADS = D_SSD // SSD_HEAD_DIM
SSD_GROUPS = 2
SSD_STATE = 128
SSD_CONV_CH = D_SSD + 2 * SSD_GROUPS * SSD_STATE
CONV_WIDTH = 4
FFN_CONV_WIDTH = 3
D_FF = 2816
IN_SIZES = (D_MLSTM, D_MLSTM, MLSTM_HEADS, MLSTM_HEADS, D_RGLRU, D_RGLRU, D_SSD, SSD_CONV_CH, SSD_HEADS)
D_IN = D_MLSTM * 2 + MLSTM_HEADS * 2 + D_RGLRU * 2 + D_SSD + SSD_CONV_CH + SSD_HEADS

kernel_name = "hymba_mlstm_rglru_ssd_convffn_sandwich"


def rms_norm(x, g):
    xf = x.astype(jnp.float32)
    y = xf * lax.rsqrt(jnp.mean(xf * xf, axis=-1, keepdims=True) + EPS)
    return (y * g.astype(jnp.float32)).astype(x.dtype)


def causal_dwconv(x, w, b):
    K = w.shape[0]
    S = x.shape[1]
    xp = jnp.pad(x, ((0, 0), (K - 1, 0), (0, 0)))
    out = b
    for k in range(K):
        out = out + xp[:, k:k + S] * w[k]
    return out


def split_cols(p):
    idx = []
    acc = 0
    for s in IN_SIZES[:-1]:
        acc += s
        idx.append(acc)
    return jnp.split(p, idx, axis=-1)


def mlstm_group(x_m, o_pre, i_pre, f_pre, conv_w, conv_b, w_q, w_k, w_v, b_i, b_f, g_norm):
    Bsz, S, _ = x_m.shape
    H, dh, L = MLSTM_HEADS, MLSTM_HEAD_DIM, CHUNK
    NC = S // L
    f32 = jnp.float32
    x_c = jax.nn.silu(causal_dwconv(x_m, conv_w, conv_b))
    xc_h = x_c.reshape(Bsz, S, H, dh)
    xm_h = x_m.reshape(Bsz, S, H, dh)
    q = jnp.einsum('bshd,hde->bhse', xc_h, w_q).astype(f32).reshape(Bsz, H, NC, L, dh)
    k = (jnp.einsum('bshd,hde->bhse', xc_h, w_k).astype(f32) * (dh ** -0.5)).reshape(Bsz, H, NC, L, dh)
    v = jnp.einsum('bshd,hde->bhse', xm_h, w_v).astype(f32).reshape(Bsz, H, NC, L, dh)
    log_i = jnp.transpose((i_pre + b_i).astype(f32), (0, 2, 1)).reshape(Bsz, H, NC, L)
    log_f = jax.nn.log_sigmoid(jnp.transpose((f_pre + b_f).astype(f32), (0, 2, 1))).reshape(Bsz, H, NC, L)
    bcum = jnp.cumsum(log_f, axis=-1)
    b_tot = bcum[..., -1]
    causal = jnp.tril(jnp.ones((L, L), dtype=bool))
    d_intra = bcum[..., :, None] - bcum[..., None, :] + log_i[..., None, :]
    d_intra = jnp.where(causal, d_intra, -jnp.inf)
    w_state = b_tot[..., None] - bcum + log_i
    m_loc = jnp.max(w_state, axis=-1)
    ek = jnp.exp(w_state - m_loc[..., None])[..., None] * k
    c_loc = jnp.einsum('bhcld,bhcle->bhcde', ek, v)
    n_loc = jnp.sum(ek, axis=-2)

    def step(carry, inp):
        c_prev, n_prev, m_prev = carry
        c_l, n_l, m_l, bt = inp
        m_new = jnp.maximum(bt + m_prev, m_l)
        s_prev = jnp.exp(bt + m_prev - m_new)
        s_loc = jnp.exp(m_l - m_new)
        c_new = s_prev[..., None, None] * c_prev + s_loc[..., None, None] * c_l
        n_new = s_prev[..., None] * n_prev + s_loc[..., None] * n_l
        return (c_new, n_new, m_new), (c_prev, n_prev, m_prev)

    init = (jnp.zeros((Bsz, H, dh, dh), f32), jnp.zeros((Bsz, H, dh), f32), jnp.zeros((Bsz, H), f32))
    xs = (jnp.moveaxis(c_loc, 2, 0), jnp.moveaxis(n_loc, 2, 0), jnp.moveaxis(m_loc, 2, 0), jnp.moveaxis(b_tot, 2, 0))
    _, (c_st, n_st, m_st) = lax.scan(step, init, xs)
    c_st = jnp.moveaxis(c_st, 0, 2)
    n_st = jnp.moveaxis(n_st, 0, 2)
    m_st = jnp.moveaxis(m_st, 0, 2)
    inter_log = bcum + m_st[..., None]
    m_t = jnp.maximum(inter_log, jnp.max(d_intra, axis=-1))
    e_inter = jnp.exp(inter_log - m_t)
    scores = jnp.einsum('bhctd,bhcsd->bhcts', q, k) * jnp.exp(d_intra - m_t[..., None])
    num = jnp.einsum('bhcts,bhcse->bhcte', scores, v) + e_inter[..., None] * jnp.einsum('bhctd,bhcde->bhcte', q, c_st)
    den = jnp.sum(scores, axis=-1) + e_inter * jnp.einsum('bhctd,bhcd->bhct', q, n_st)
    h = num / jnp.maximum(jnp.abs(den), jnp.exp(-m_t))[..., None]
    h = jnp.transpose(h.reshape(Bsz, H, S, dh), (0, 2, 1, 3))
    h = jax.nn.sigmoid(o_pre.astype(f32)).reshape(Bsz, S, H, dh) * h
    h = h * lax.rsqrt(jnp.mean(h * h, axis=-1, keepdims=True) + EPS)
    h = h * g_norm.astype(f32).reshape(H, dh)
    return h.reshape(Bsz, S, D_MLSTM).astype(x_m.dtype)


def rglru_group(x_r, y_r, conv_w, conv_b, w_a, b_a, w_x, b_x, lam):
    Bsz, S, _ = x_r.shape
    f32 = jnp.float32
    xc = causal_dwconv(x_r, conv_w, conv_b)
    xb = xc.reshape(Bsz, S, RGLRU_BLOCKS, RGLRU_BLOCK_DIM)
    r = jax.nn.sigmoid(jnp.einsum('bsnd,nde->bsne', xb, w_a).reshape(Bsz, S, D_RGLRU) + b_a)
    i = jax.nn.sigmoid(jnp.einsum('bsnd,nde->bsne', xb, w_x).reshape(Bsz, S, D_RGLRU) + b_x)
    log_a = -RGLRU_C * r.astype(f32) * jax.nn.softplus(-lam.astype(f32))
    a = jnp.exp(log_a)
    u = jnp.sqrt(-jnp.expm1(2.0 * log_a)) * (i * xc).astype(f32)

    def combine(left, right):
        a1, b1 = left
        a2, b2 = right
        return a2 * a1, a2 * b1 + b2

    _, h = lax.associative_scan(combine, (a, u), axis=1)
    return h.astype(x_r.dtype) * jax.nn.gelu(y_r)


def ssd_group(z, xbc, dt_raw, conv_w, conv_b, dt_bias, a_log, d_skip, g_norm):
    Bsz, S, _ = z.shape
    H, P, G, N, L = SSD_HEADS, SSD_HEAD_DIM, SSD_GROUPS, SSD_STATE, CHUNK
    NC = S // L
    f32 = jnp.float32
    xbc = jax.nn.silu(causal_dwconv(xbc, conv_w, conv_b))
    xs, Bm, Cm = jnp.split(xbc, [D_SSD, D_SSD + G * N], axis=-1)
    xs = xs.reshape(Bsz, S, H, P)
    Bh = jnp.repeat(Bm.reshape(Bsz, S, G, N), H // G, axis=2)
    Ch = jnp.repeat(Cm.reshape(Bsz, S, G, N), H // G, axis=2)
    dt = jax.nn.softplus(dt_raw.astype(f32) + dt_bias.astype(f32))
    a = dt * (-jnp.exp(a_log.astype(f32)))
    x_c = xs.reshape(Bsz, NC, L, H, P)
    Bc = Bh.reshape(Bsz, NC, L, H, N)
    Cc = Ch.reshape(Bsz, NC, L, H, N)
    xdt = x_c * dt.reshape(Bsz, NC, L, H)[..., None]
    a_cum = jnp.cumsum(jnp.transpose(a.reshape(Bsz, NC, L, H), (0, 3, 1, 2)), axis=-1)
    causal = jnp.tril(jnp.ones((L, L), dtype=bool))
    seg = jnp.where(causal, a_cum[..., :, None] - a_cum[..., None, :], -jnp.inf)
    cb = jnp.einsum('bclhn,bcshn->bhcls', Cc, Bc) * jnp.exp(seg)
    y_diag = jnp.einsum('bhcls,bcshp->bclhp', cb, xdt)
    decay_states = jnp.exp(a_cum[..., -1:] - a_cum)
    states = jnp.einsum('bclhn,bhcl,bclhp->bchpn', Bc, decay_states, xdt)
    chunk_decay = jnp.exp(a_cum[..., -1])

    def step(carry, inp):
        st, dec = inp
        return dec[..., None, None] * carry + st, carry

    init = jnp.zeros((Bsz, H, P, N), states.dtype)
    _, s_start = lax.scan(step, init, (jnp.moveaxis(states, 1, 0), jnp.moveaxis(chunk_decay, 2, 0)))
    s_start = jnp.moveaxis(s_start, 0, 1)
    y_off = jnp.einsum('bclhn,bchpn,bhcl->bclhp', Cc, s_start, jnp.exp(a_cum))
    y = (y_diag + y_off).reshape(Bsz, S, H, P) + xs * d_skip[:, None]
    y = y.reshape(Bsz, S, D_SSD).astype(z.dtype)
    return rms_norm(y * jax.nn.silu(z), g_norm)


def conv_ffn(x, w_up, conv_w, conv_b, w_down):
    u = causal_dwconv(x @ w_up, conv_w, conv_b)
    g, v = jnp.split(u, 2, axis=-1)
    return (jax.nn.gelu(g) * v) @ w_down


def setup_inputs(seed: int = 0) -> dict:
    key = jax.random.key(seed)
    ks = jax.random.split(key, 40)
    f32 = jnp.float32

    def nrm(k, shape, scale):
        return jax.random.normal(k, shape, f32) * scale

    def gain(k, n):
        return 1.0 + 0.02 * jax.random.normal(k, (DEPTH, n), f32)

    u_a = jax.random.uniform(ks[20], (DEPTH, D_RGLRU), f32, minval=0.9, maxval=0.999)
    a_base = u_a ** (1.0 / RGLRU_C)
    dt0 = jnp.exp(jax.random.uniform(ks[24], (DEPTH, SSD_HEADS), f32, minval=math.log(1e-3), maxval=math.log(1e-1)))
    return {
        "x": jax.random.normal(ks[0], (BATCH, SEQ, D_MODEL), f32),
        "norm_mix_pre": gain(ks[1], D_MODEL),
        "norm_mix_post": gain(ks[2], D_MODEL),
        "norm_ffn_pre": gain(ks[3], D_MODEL),
        "norm_ffn_post": gain(ks[4], D_MODEL),
        "w_in": nrm(ks[5], (DEPTH, D_MODEL, D_IN), D_MODEL ** -0.5),
        "conv_m_w": nrm(ks[6], (DEPTH, CONV_WIDTH, D_MLSTM), CONV_WIDTH ** -0.5),
        "conv_m_b": nrm(ks[7], (DEPTH, D_MLSTM), 0.02),
        "w_q_m": nrm(ks[8], (DEPTH, MLSTM_HEADS, MLSTM_HEAD_DIM, MLSTM_HEAD_DIM), MLSTM_HEAD_DIM ** -0.5),
        "w_k_m": nrm(ks[9], (DEPTH, MLSTM_HEADS, MLSTM_HEAD_DIM, MLSTM_HEAD_DIM), MLSTM_HEAD_DIM ** -0.5),
        "w_v_m": nrm(ks[10], (DEPTH, MLSTM_HEADS, MLSTM_HEAD_DIM, MLSTM_HEAD_DIM), MLSTM_HEAD_DIM ** -0.5),
        "b_i_m": nrm(ks[11], (DEPTH, MLSTM_HEADS), 0.1),
        "b_f_m": jnp.linspace(3.0, 6.0, MLSTM_HEADS, dtype=f32)[None, :] + nrm(ks[12], (DEPTH, MLSTM_HEADS), 0.1),
        "norm_m": gain(ks[13], D_MLSTM),
        "conv_r_w": nrm(ks[14], (DEPTH, CONV_WIDTH, D_RGLRU), CONV_WIDTH ** -0.5),
        "conv_r_b": nrm(ks[15], (DEPTH, D_RGLRU), 0.02),
        "w_a_r": nrm(ks[16], (DEPTH, RGLRU_BLOCKS, RGLRU_BLOCK_DIM, RGLRU_BLOCK_DIM), RGLRU_BLOCK_DIM ** -0.5),
        "b_a_r": nrm(ks[17], (DEPTH, D_RGLRU), 0.02),
        "w_x_r": nrm(ks[18], (DEPTH, RGLRU_BLOCKS, RGLRU_BLOCK_DIM, RGLRU_BLOCK_DIM), RGLRU_BLOCK_DIM ** -0.5),
        "b_x_r": nrm(ks[19], (DEPTH, D_RGLRU), 0.02),
        "lam_r": jnp.log(a_base) - jnp.log1p(-a_base),
        "conv_s_w": nrm(ks[21], (DEPTH, CONV_WIDTH, SSD_CONV_CH), CONV_WIDTH ** -0.5),
        "conv_s_b": nrm(ks[22], (DEPTH, SSD_CONV_CH), 0.02),
        "dt_bias_s": dt0 + jnp.log(-jnp.expm1(-dt0)),
        "a_log_s": jnp.log(jax.random.uniform(ks[25], (DEPTH, SSD_HEADS), f32, minval=1.0, maxval=16.0)),
        "d_skip_s": 1.0 + nrm(ks[26], (DEPTH, SSD_HEADS), 0.1),
        "norm_s": gain(ks[27], D_SSD),
        "w_out": nrm(ks[28], (DEPTH, D_MIX, D_MODEL), D_MIX ** -0.5),
        "w_up": nrm(ks[29], (DEPTH, D_MODEL, 2 * D_FF), D_MODEL ** -0.5),
        "conv_f_w": nrm(ks[30], (DEPTH, FFN_CONV_WIDTH, 2 * D_FF), FFN_CONV_WIDTH ** -0.5),
        "conv_f_b": nrm(ks[31], (DEPTH, 2 * D_FF), 0.02),
        "w_down": nrm(ks[32], (DEPTH, D_FF, D_MODEL), D_FF ** -0.5),
    }


def reference(x, norm_mix_pre, norm_mix_post, norm_ffn_pre, norm_ffn_post, w_in,
              conv_m_w, conv_m_b, w_q_m, w_k_m, w_v_m, b_i_m, b_f_m, norm_m,
              conv_r_w, conv_r_b, w_a_r, b_a_r, w_x_r, b_x_r, lam_r,
              conv_s_w, conv_s_b, dt_bias_s, a_log_s, d_skip_s, norm_s,
              w_out, w_up, conv_f_w, conv_f_b, w_down):
    for l in range(DEPTH):
        h = rms_norm(x, norm_mix_pre[l])
        xm, om, im, fm, xr, yr, zs, xbcs, dts = split_cols(h @ w_in[l])
        y_m = mlstm_group(xm, om, im, fm, conv_m_w[l], conv_m_b[l], w_q_m[l], w_k_m[l], w_v_m[l],
                          b_i_m[l], b_f_m[l], norm_m[l])
        y_r = rglru_group(xr, yr, conv_r_w[l], conv_r_b[l], w_a_r[l], b_a_r[l], w_x_r[l], b_x_r[l], lam_r[l])
        y_s = ssd_group(zs, xbcs, dts, conv_s_w[l], conv_s_b[l], dt_bias_s[l], a_log_s[l], d_skip_s[l], norm_s[l])
        mix = jnp.concatenate([y_m, y_r, y_s], axis=-1) @ w_out[l]
        x = x + rms_norm(mix, norm_mix_post[l])
        h = rms_norm(x, norm_ffn_pre[l])
        x = x + rms_norm(conv_ffn(h, w_up[l], conv_f_w[l], conv_f_b[l], w_down[l]), norm_ffn_post[l])
    return x
```

```python
import contextlib
import numpy as np
import concourse.bass as bass
import concourse.mybir as mybir
from concourse.bass_utils import run_bass_kernel_spmd

F32 = mybir.dt.float32
F32R = mybir.dt.float32r
AF = mybir.ActivationFunctionType
ALU = mybir.AluOpType

SAME_ENGINE_SYNC = True
HALO_ENG = "dve"


class Buf:
    __slots__ = ("name", "lastw", "readers", "dsem", "dcount", "psum")

    def __init__(self, name, psum=False):
        self.name = name
        self.psum = psum
        self.lastw = None
        self.readers = []
        self.dsem = None
        self.dcount = 0


class V:
    __slots__ = ("buf", "ap")

    def __init__(self, buf, ap):
        self.buf = buf
        self.ap = ap

    def __getitem__(self, k):
        return V(self.buf, self.ap[k])

    def rr(self, s, **kw):
        return V(self.buf, self.ap.rearrange(s, **kw))

    def bc(self, shape):
        return V(self.buf, self.ap.to_broadcast(list(shape)))

    def us(self, axis):
        return V(self.buf, self.ap.unsqueeze(axis))

    def r(self):
        return V(self.buf, self.ap.bitcast(F32R))

    def f(self):
        return V(self.buf, self.ap.bitcast(F32))

    @property
    def shape(self):
        return self.ap.shape


def _ap(x):
    return x.ap if isinstance(x, V) else x


class Prog:
    ENGS = ("pe", "act", "dve", "pool", "sp")

    def __init__(self, nc):
        self.nc = nc
        self.stack = contextlib.ExitStack()
        self.ops = {e: [] for e in self.ENGS}
        self.sems = {}
        self.ecount = {e: 0 for e in self.ENGS}
        self.seen = {e: {} for e in self.ENGS}
        for e in ("pe", "act", "dve", "pool"):
            self.sems[e] = self.stack.enter_context(nc.semaphore("e_" + e))
        self.ctotal = {}
        for q in ("sp", "pool"):
            self.sems["const_" + q] = self.stack.enter_context(nc.semaphore("consts_" + q))
            self.ctotal[q] = 0
        self.n_inst = 0

    def sbuf(self, name, shape, dtype=F32):
        t = self.stack.enter_context(self.nc.sbuf_tensor(name, list(shape), dtype))
        return V(Buf(name), t[:])

    def psum(self, name, shape, dtype=F32):
        t = self.stack.enter_context(self.nc.psum_tensor(name, list(shape), dtype))
        return V(Buf(name, psum=True), t[:])

    def _collect(self, eng, reads, writes):
        need = {}

        def add(ev):
            if ev is None:
                return
            k, val = ev
            if need.get(k, 0) < val:
                need[k] = val
        for b in reads:
            add(b.lastw)
            if b.psum:
                for ev in b.readers:
                    if ev[0] != eng:
                        add(ev)
        for b in writes:
            add(b.lastw)
            for ev in b.readers:
                add(ev)
        waits = []
        for k, val in need.items():
            if k == eng and (eng == "pe" or not SAME_ENGINE_SYNC):
                continue
            if self.seen[eng].get(k, 0) >= val:
                continue
            self.seen[eng][k] = val
            waits.append((k, val))
        return waits

    def _commit(self, ev, reads, writes):
        for b in writes:
            b.lastw = ev
            b.readers = []
        for b in reads:
            if b in writes:
                continue
            b.readers = [r for r in b.readers if r[0] != ev[0]] + [ev]

    def op(self, eng, fn, reads, writes):
        reads = list({id(v.buf): v.buf for v in reads if isinstance(v, V)}.values())
        writes = list({id(v.buf): v.buf for v in writes if isinstance(v, V)}.values())
        waits = self._collect(eng, reads, writes)
        self.ecount[eng] += 1
        ev = (eng, self.ecount[eng])
        self.ops[eng].append((waits, fn, (eng, 1)))
        self._commit(ev, reads, writes)
        self.n_inst += 1

    def dma(self, out, in_, queue="sp", const=False):
        reads = [in_.buf] if isinstance(in_, V) else []
        writes = [out.buf] if isinstance(out, V) else []
        waits = self._collect(queue, reads, writes)
        o, i = _ap(out), _ap(in_)
        fn = lambda e, o=o, i=i: e.dma_start(out=o, in_=i)
        if const:
            self.ctotal[queue] += 16
            self.ops[queue].append((waits, fn, ("const_" + queue, 16), True))
            return None
        b = (writes + reads)[0]
        if b.dsem is None:
            key = "d%d" % len(self.sems)
            self.sems[key] = self.stack.enter_context(self.nc.semaphore(key))
            b.dsem = key
        b.dcount += 16
        ev = (b.dsem, b.dcount)
        self.ops[queue].append((waits, fn, (b.dsem, 16)))
        self._commit(ev, reads, writes)
        self.n_inst += 1
        return ev

    def mm(self, out, lhsT, rhs, start=True, stop=True, r32=True):
        l = lhsT.ap.bitcast(F32R) if r32 else lhsT.ap
        r = rhs.ap.bitcast(F32R) if r32 else rhs.ap
        o = out.ap
        self.op("pe", lambda e: e.matmul(o, l, r, start=start, stop=stop), [lhsT, rhs], [out])

    def transpose(self, out, in_, ident):
        o, i, d = out.ap, in_.ap, ident.ap
        self.op("pe", lambda e: e.transpose(o, i, d), [in_, ident], [out])

    def act(self, out, in_, func, bias=0.0, scale=1.0):
        o, i = out.ap, in_.ap
        b, s = _ap(bias), _ap(scale)
        rd = [in_] + [x for x in (bias, scale) if isinstance(x, V)]
        self.op("act", lambda e: e.activation(out=o, in_=i, func=func, bias=b, scale=s), rd, [out])

    def tt(self, out, in0, in1, op, eng="dve"):
        o, a, b = out.ap, in0.ap, in1.ap
        self.op(eng, lambda e: e.tensor_tensor(out=o, in0=a, in1=b, op=op), [in0, in1], [out])

    def ts(self, out, in0, s1, op0, s2=None, op1=None, eng="dve"):
        o, a = out.ap, in0.ap
        x1, x2 = _ap(s1), _ap(s2)
        rd = [in0] + [x for x in (s1, s2) if isinstance(x, V)]
        if op1 is None:
            self.op(eng, lambda e: e.tensor_scalar(out=o, in0=a, scalar1=x1, scalar2=None, op0=op0), rd, [out])
        else:
            self.op(eng, lambda e: e.tensor_scalar(out=o, in0=a, scalar1=x1, scalar2=x2, op0=op0, op1=op1),
                    rd, [out])

    def stt(self, out, in0, scalar, in1, op0, op1):
        o, a, b = out.ap, in0.ap, in1.ap
        s = _ap(scalar)
        rd = [in0, in1] + ([scalar] if isinstance(scalar, V) else [])
        self.op("dve", lambda e: e.scalar_tensor_tensor(out=o, in0=a, scalar=s, in1=b, op0=op0, op1=op1),
                rd, [out])

    def copy(self, out, in_, eng="dve"):
        o, i = out.ap, in_.ap
        if eng == "act":
            self.op("act", lambda e: e.activation(out=o, in_=i, func=AF.Copy), [in_], [out])
        else:
            self.op(eng, lambda e: e.tensor_copy(out=o, in_=i), [in_], [out])

    def recip(self, out, in_):
        o, i = out.ap, in_.ap
        self.op("dve", lambda e: e.reciprocal(out=o, in_=i), [in_], [out])

    def scan(self, out, d0, d1, initial, op0, op1):
        o, a, b = out.ap, d0.ap, d1.ap
        ini = _ap(initial)
        rd = [d0, d1] + ([initial] if isinstance(initial, V) else [])
        self.op("dve", lambda e: e.tensor_tensor_scan(out=o, data0=a, data1=b, initial=ini, op0=op0, op1=op1),
                rd, [out])

    def memset(self, out, val, eng="dve"):
        o = out.ap
        self.op(eng, lambda e: e.memset(o, val), [], [out])

    def emit(self, final_events=()):
        nc = self.nc
        sems = self.sems
        ops = self.ops
        ctotal = self.ctotal

        def run(e, name):
            first = True
            for rec in ops[name]:
                waits, fn, inc = rec[:3]
                if first and len(rec) == 3:
                    for q, tot in ctotal.items():
                        if tot > 0:
                            e.wait_ge(sems["const_" + q], tot)
                    first = False
                for k, val in waits:
                    e.wait_ge(sems[k], val)
                fn(e).then_inc(sems[inc[0]], inc[1])
            if name == "sp":
                for k, val in final_events:
                    e.wait_ge(sems[k], val)

        with nc.Block() as block:
            @block.sync
            def _(e):
                run(e, "sp")

            @block.tensor
            def _(e):
                run(e, "pe")

            @block.scalar
            def _(e):
                run(e, "act")

            @block.vector
            def _(e):
                run(e, "dve")

            @block.gpsimd
            def _(e):
                run(e, "pool")
        self.stack.close()


D = 1024
S = 4096
DEPTH = 2
T = 512
NT = S // T
LC = 64
NCH = T // LC
DFF = 2816
NJ = DFF // 128
EPS = 1e-6
HALO = 3
SW = T + HALO

C_XM, C_OM, C_IM, C_FM, C_XR, C_YR, C_ZS, C_XBC, C_DT = 0, 512, 1024, 1028, 1032, 1544, 2056, 2568, 3592

WB = {}
_off = 0


def _wb(name, kc, ncols=128):
    global _off
    WB[name] = (_off, kc, ncols)
    _off += kc * ncols


for _i in range(4):
    _wb("xm%d" % _i, 8)
for _i in range(4):
    _wb("om%d" % _i, 8)
_wb("gates", 8, 16)
for _i in range(4):
    _wb("xr%d" % _i, 8)
for _i in range(4):
    _wb("yr%d" % _i, 8)
for _i in range(4):
    _wb("zs%d" % _i, 8)
for _i in range(8):
    _wb("xbc%d" % _i, 8)
for _n in range(8):
    _wb("out%da" % _n, 8)
    _wb("out%db" % _n, 4)
for _j in range(NJ):
    _wb("upg%d" % _j, 8)
    _wb("upv%d" % _j, 8)
for _n in range(8):
    _wb("dn%da" % _n, 8)
    _wb("dn%db" % _n, 8)
    _wb("dn%dc" % _n, 6)
WCOLS = _off
WSEQ = (["xm%d" % i for i in range(4)] + ["gates"] + ["om%d" % i for i in range(4)]
        + ["xr%d" % i for i in range(4)] + ["yr%d" % i for i in range(4)]
        + ["xbc%d" % i for i in range(8)] + ["zs%d" % i for i in range(4)]
        + [n for n in WB if n.startswith("out")] + [n for n in WB if n.startswith("up")]
        + [n for n in WB if n.startswith("dn")])
assert len(WSEQ) == len(WB)

WS_Q, WS_K, WS_V, WS_A, WS_X = 0, 512, 1024, 1536, 2048
WSCOLS = 2560

CV = {}
_cvo = 0


def _cv(name, n):
    global _cvo
    CV[name] = _cvo
    _cvo += n


for _n, _k in [("g_mix_pre", 8), ("g_mix_post", 8), ("g_ffn_pre", 8), ("g_ffn_post", 8),
               ("cmw", 16), ("cmb", 4), ("norm_m", 4),
               ("crw", 16), ("crb", 4), ("b_a", 4), ("b_x", 4), ("lam", 4),
               ("csw", 32), ("csb", 8), ("norm_s", 4), ("dskip", 4),
               ("cfw", 3 * 44), ("cfb", 44),
               ("b_i", 1), ("b_f", 1), ("dt_bias", 1), ("a_log", 1)]:
    _cv(_n, _k)
NCV = _cvo

CM_ID, CM_ONES, CM_MASK, CM_RESET, CM_SEL = 0, 128, 256, 320, 832
NCM = 832


def host_consts():
    cm = np.zeros((128, NCM), np.float32)
    cm[:, CM_ID:CM_ID + 128] = np.eye(128, dtype=np.float32)
    cm[:, CM_ONES:CM_ONES + 128] = 1.0
    s = np.arange(64)[:, None]
    t = np.arange(64)[None, :]
    cm[:64, CM_MASK:CM_MASK + 64] = (s <= t).astype(np.float32)
    r = np.ones(512, np.float32)
    r[::64] = 0.0
    cm[:, CM_RESET:CM_RESET + 512] = r[None, :]
    return cm


def _pc(v, n):
    return np.ascontiguousarray(np.asarray(v, np.float32).reshape(n, 128).T)


def _blk(W, col0, kc, ncols=128, row0=0):
    sub = W[row0:row0 + kc * 128, col0:col0 + ncols]
    return np.ascontiguousarray(sub.reshape(kc, 128, ncols).transpose(1, 0, 2).reshape(128, kc * ncols))


def host_layout(inp):
    wbig, wsm = [], []
    cvec = np.zeros((DEPTH, 128, NCV), np.float32)
    for l in range(DEPTH):
        w_in, w_out, w_up, w_dn = (np.asarray(inp[k][l], np.float32) for k in ("w_in", "w_out", "w_up", "w_down"))
        big = np.zeros((128, WCOLS), np.float32)

        def put(name, arr):
            o, kc, ncols = WB[name]
            big[:, o:o + kc * ncols] = arr
        for i in range(4):
            put("xm%d" % i, _blk(w_in, C_XM + i * 128, 8))
            put("om%d" % i, _blk(w_in, C_OM + i * 128, 8))
            put("xr%d" % i, _blk(w_in, C_XR + i * 128, 8))
            put("yr%d" % i, _blk(w_in, C_YR + i * 128, 8))
            put("zs%d" % i, _blk(w_in, C_ZS + i * 128, 8))
        for i in range(8):
            put("xbc%d" % i, _blk(w_in, C_XBC + i * 128, 8))
        gcols = np.concatenate([w_in[:, C_IM:C_IM + 4], w_in[:, C_FM:C_FM + 4], w_in[:, C_DT:C_DT + 8]], axis=1)
        put("gates", _blk(gcols, 0, 8, 16))
        for n in range(8):
            put("out%da" % n, _blk(w_out, n * 128, 8, row0=0))
            put("out%db" % n, _blk(w_out, n * 128, 4, row0=1024))
            put("dn%da" % n, _blk(w_dn, n * 128, 8, row0=0))
            put("dn%db" % n, _blk(w_dn, n * 128, 8, row0=1024))
            put("dn%dc" % n, _blk(w_dn, n * 128, 6, row0=2048))
        for j in range(NJ):
            put("upg%d" % j, _blk(w_up, j * 128, 8))
            put("upv%d" % j, _blk(w_up, DFF + j * 128, 8))
        wbig.append(big)
        sm = np.zeros((128, WSCOLS), np.float32)
        for h in range(4):
            sm[:, WS_Q + h * 128: WS_Q + (h + 1) * 128] = inp["w_q_m"][l][h]
            sm[:, WS_K + h * 128: WS_K + (h + 1) * 128] = inp["w_k_m"][l][h]
            sm[:, WS_V + h * 128: WS_V + (h + 1) * 128] = inp["w_v_m"][l][h]
        for j in range(4):
            for q in range(2):
                sm[q * 64:(q + 1) * 64, WS_A + j * 128 + q * 64: WS_A + j * 128 + (q + 1) * 64] = inp["w_a_r"][l][2 * j + q]
                sm[q * 64:(q + 1) * 64, WS_X + j * 128 + q * 64: WS_X + j * 128 + (q + 1) * 64] = inp["w_x_r"][l][2 * j + q]
        wsm.append(sm)
        cv = cvec[l]
        for nm, key in [("g_mix_pre", "norm_mix_pre"), ("g_mix_post", "norm_mix_post"),
                        ("g_ffn_pre", "norm_ffn_pre"), ("g_ffn_post", "norm_ffn_post")]:
            cv[:, CV[nm]:CV[nm] + 8] = _pc(inp[key][l], 8)
        for k in range(4):
            cv[:, CV["cmw"] + k * 4: CV["cmw"] + k * 4 + 4] = _pc(inp["conv_m_w"][l][k], 4)
            cv[:, CV["crw"] + k * 4: CV["crw"] + k * 4 + 4] = _pc(inp["conv_r_w"][l][k], 4)
            cv[:, CV["csw"] + k * 8: CV["csw"] + k * 8 + 8] = _pc(inp["conv_s_w"][l][k], 8)
        for k in range(3):
            cv[:, CV["cfw"] + k * 44: CV["cfw"] + k * 44 + 44] = _pc(inp["conv_f_w"][l][k], 44)
        cv[:, CV["cfb"]:CV["cfb"] + 44] = _pc(inp["conv_f_b"][l], 44)
        cv[:, CV["cmb"]:CV["cmb"] + 4] = _pc(inp["conv_m_b"][l], 4)
        cv[:, CV["norm_m"]:CV["norm_m"] + 4] = _pc(inp["norm_m"][l], 4)
        cv[:, CV["crb"]:CV["crb"] + 4] = _pc(inp["conv_r_b"][l], 4)
        cv[:, CV["b_a"]:CV["b_a"] + 4] = _pc(inp["b_a_r"][l], 4)
        cv[:, CV["b_x"]:CV["b_x"] + 4] = _pc(inp["b_x_r"][l], 4)
        cv[:, CV["lam"]:CV["lam"] + 4] = _pc(inp["lam_r"][l], 4)
        cv[:, CV["csb"]:CV["csb"] + 8] = _pc(inp["conv_s_b"][l], 8)
        cv[:, CV["norm_s"]:CV["norm_s"] + 4] = _pc(inp["norm_s"][l], 4)
        cv[:, CV["dskip"]:CV["dskip"] + 4] = _pc(np.repeat(np.asarray(inp["d_skip_s"][l], np.float32), 64), 4)
        cv[0:4, CV["b_i"]] = inp["b_i_m"][l]
        cv[0:4, CV["b_f"]] = inp["b_f_m"][l]
        cv[0:8, CV["dt_bias"]] = inp["dt_bias_s"][l]
        cv[0:8, CV["a_log"]] = inp["a_log_s"][l]
    return wbig, wsm, cvec


def build(n_tiles=NT, depth=DEPTH, dbg=None, stop=None):
    nc = bass.Bass("TRN2", target_bir_lowering=False)
    xT = nc.dram_tensor("xT", [D, S], F32, kind="ExternalInput").ap()
    outT = nc.dram_tensor("outT", [D, S], F32, kind="ExternalOutput").ap()
    wbig_d = [nc.dram_tensor("wbig%d" % l, [128, WCOLS], F32, kind="ExternalInput").ap() for l in range(DEPTH)]
    wsm_d = [nc.dram_tensor("wsm%d" % l, [128, WSCOLS], F32, kind="ExternalInput").ap() for l in range(DEPTH)]
    cvec_d = nc.dram_tensor("cvec", [DEPTH, 128, NCV], F32, kind="ExternalInput").ap()
    cmat_d = nc.dram_tensor("cmat", [128, NCM], F32, kind="ExternalInput").ap()
    dbg = dbg or {}
    dbg_d = {}
    for name, (dt_, dl_, ncols) in dbg.items():
        dbg_d[name] = nc.dram_tensor("dbg_" + name, [128, ncols], F32, kind="ExternalOutput").ap()

    P = Prog(nc)
    final_evs = []

    CM = P.sbuf("CM", [128, NCM])
    P.dma(CM, cmat_d, const=True)
    ONESR = P.sbuf("ONESR", [128, 128], F32R)
    P.dma(ONESR, cmat_d[:, CM_ONES:CM_ONES + 128], queue="pool", const=True)
    IDENT = CM[:, CM_ID:CM_ID + 128]
    MASKT = CM[0:64, CM_MASK:CM_MASK + 64]
    RESET = CM[:, CM_RESET:CM_RESET + 512]

    def bcast_row(ps, rows, h, k, tmp):
        P.act(tmp[0:k, :], rows[0:k, :], AF.Copy, scale=IDENT[0:k, h:h + 1])
        P.mm(ps, CM[0:k, CM_ONES:CM_ONES + 128], tmp[0:k, :], r32=False)
    CVt = []
    for l in range(depth):
        cv = P.sbuf("CV%d" % l, [128, NCV])
        P.dma(cv, cvec_d[l], const=True)
        CVt.append(cv)
    WSt = P.sbuf("WS", [128, WSCOLS], F32R)

    def cvc(l, name, c=0, rows=128):
        o = CV[name] + c
        return CVt[l][0:rows, o:o + 1]

    X = [P.sbuf("X%d" % c, [128, T]) for c in range(8)]
    H = [P.sbuf("H%d" % c, [128, T]) for c in range(8)]
    RS_ = [P.sbuf("RS%d" % i, [128, SW]) for i in range(28)]
    FS = [P.sbuf("FS%d" % i, [128, SW]) for i in range(12)]
    NRING = 5
    RING = [P.sbuf("WR%d" % i, [128, 1024], F32R) for i in range(NRING)]
    TMP = [P.sbuf("TMP%d" % i, [128, T]) for i in range(3)]
    SQ = [P.sbuf("SQ%d" % i, [128, T]) for i in range(2)]
    ROW = [P.sbuf("ROW%d" % i, [8, T]) for i in range(5)]
    GT = P.sbuf("GT", [64, NCH * 32])
    TK = [P.sbuf("TK%d" % i, [64, 256]) for i in range(6)]
    AT_ = [P.sbuf("AT%d" % i, [64, 64]) for i in range(4)]
    MT = [P.sbuf("MT%d" % i, [64, T]) for i in range(4)]
    SEGT = P.sbuf("SEGT", [64, T])
    CBM = P.sbuf("CBM", [64, T])
    CSTMP = [P.sbuf("CSTMP%d" % i, [128, 256]) for i in range(2)]
    HF = [P.sbuf("HF%d" % l, [128, 44 * 2]) for l in range(depth)]
    HMR = [P.sbuf("HMR%d" % l, [128, 4 * 3]) for l in range(depth)]
    HM = [P.sbuf("HM%d" % l, [128, 12 * 3]) for l in range(depth)]
    CS = [[P.sbuf("CS%d_%d" % (l, h), [128, 256]) for h in range(4)] for l in range(depth)]
    SST = [[P.sbuf("SST%d_%d" % (l, j), [128, 128]) for j in range(4)] for l in range(depth)]
    HST = [P.sbuf("HST%d" % l, [128, 4]) for l in range(depth)]
    NCF = [P.sbuf("NCF%d" % l, [128, 4]) for l in range(depth)]
    NBF = [P.sbuf("NBF%d" % l, [8, 2]) for l in range(depth)]
    SMALL = P.sbuf("SMALL", [128, 64])

    PS = [P.psum("PS%d" % i, [128, T]) for i in range(8)]
    dense_rr = [0]
    small_rr = [0]

    dense_banks = [[0, 1]]

    def ps_dense():
        dense_rr[0] += 1
        lst = dense_banks[0]
        return PS[lst[dense_rr[0] % len(lst)]]

    def ps_small():
        small_rr[0] ^= 1
        return PS[2 + small_rr[0]]
    PL = PS[4:8]

    for l in range(depth):
        P.memset(HF[l], 0.0)
        P.memset(HM[l], 0.0)
        P.ts(HMR[l].r(), CM[:, 0:12], 0.0, ALU.mult)
        P.memset(HST[l], 0.0)
        for h in range(4):
            P.ts(CS[l][h].r(), CM[:, 0:256], 0.0, ALU.mult)
            P.ts(SST[l][h].r(), CM[:, 0:128], 0.0, ALU.mult)
        P.act(SMALL[:, 0:4], CVt[l][:, CV["lam"]:CV["lam"] + 4], AF.Exp, scale=-1.0)
        P.act(SMALL[:, 4:8], SMALL[:, 0:4], AF.Ln, bias=1.0)
        P.ts(NCF[l], SMALL[:, 4:8], -8.0, ALU.mult)
        P.ts(NBF[l][0:4, 0:1], cvc(l, "b_f", rows=4), -1.0, ALU.mult)
        P.act(NBF[l][0:8, 1:2], cvc(l, "a_log", rows=8), AF.Exp)

    total_blocks = n_tiles * depth * len(WSEQ)
    wstate = {"issued": 0, "used": 0}

    def _issue(k):
        tl, bi = divmod(k, len(WSEQ))
        l = tl % depth
        o, kc, ncols = WB[WSEQ[bi]]
        slot = RING[k % NRING]
        P.dma(slot[:, 0:kc * ncols], wbig_d[l][:, o:o + kc * ncols], queue="pool")

    def wget(l, name):
        k = wstate["used"]
        tl, bi = divmod(k, len(WSEQ))
        assert WSEQ[bi] == name and tl % depth == l, (WSEQ[bi], name, tl, l)
        while wstate["issued"] < min(total_blocks, k + NRING):
            _issue(wstate["issued"])
            wstate["issued"] += 1
        wstate["used"] += 1
        o, kc, ncols = WB[name]
        return RING[k % NRING], kc, ncols

    def dump(name, ti, l, views):
        if name in dbg and dbg[name][0] == ti and dbg[name][1] == l:
            for i, v in enumerate(views):
                n = v.shape[1]
                final_evs.append(P.dma(dbg_d[name][0:v.shape[0], i * n:(i + 1) * n], v))

    def rmsnorm_stats(src, nchunks, dim, rs_out):
        ps = ps_dense()
        for c in range(nchunks):
            sq = SQ[c % 2]
            if c % 2 == 0:
                P.act(sq.r(), src[c], AF.Square)
            else:
                P.tt(sq.r(), src[c], src[c], ALU.mult)
            P.mm(ps, ONESR, sq, start=(c == 0), stop=(c == nchunks - 1))
        P.act(rs_out, ps, AF.Ln, bias=EPS, scale=1.0 / dim)
        P.act(rs_out, rs_out, AF.Exp, scale=-0.5)

    def project(l, name, rhs_list, out_ps, M=128, lcol0=0, first=True, last=True, blk=None):
        slot, kc, ncols = blk if blk is not None else wget(l, name)
        for c in range(kc):
            P.mm(out_ps, slot[:, c * ncols + lcol0: c * ncols + lcol0 + M], rhs_list[c],
                 start=(first and c == 0), stop=(last and c == kc - 1))
        return slot, kc, ncols

    def conv4(l, acc, out_final, src_slot, wname, bname, c, nch, halo_v, src_r):
        P.copy(src_slot[:, 0:HALO].r() if src_r else src_slot[:, 0:HALO], halo_v)
        w = lambda k: CVt[l][:, CV[wname] + k * nch + c: CV[wname] + k * nch + c + 1]
        b = CVt[l][:, CV[bname] + c: CV[bname] + c + 1]
        P.ts(acc, src_slot[:, 3:3 + T], w(3), ALU.mult, b, ALU.add)
        for k in (2, 1):
            P.stt(acc, src_slot[:, k:k + T], w(k), acc, ALU.mult, ALU.add)
        P.stt(out_final, src_slot[:, 0:T], w(0), acc, ALU.mult, ALU.add)
        P.copy(halo_v.r() if src_r else halo_v, src_slot[:, T:T + 3])

    for ti in range(n_tiles):
        t0 = ti * T
        for c in range(8):
            P.dma(X[c], xT[c * 128:(c + 1) * 128, t0:t0 + T])
        for l in range(depth):
            P.dma(WSt[:, 0:1280], wsm_d[l][:, 0:1280], queue="pool")
            P.dma(WSt[:, 1280:2560], wsm_d[l][:, 1280:2560], queue="pool")
            RSt = TMP[2]
            dense_banks[0] = [0, 1]
            rmsnorm_stats(X, 8, D, RSt)
            for c in range(8):
                P.stt(H[c].r(), X[c], cvc(l, "g_mix_pre", c), RSt, ALU.mult, ALU.mult)
            dump("h", ti, l, H)
            if stop == "h":
                break
            YM = RS_[0:4]
            YR = RS_[4:8]
            YS = RS_[8:12]
            W_ = RS_[12:28]

            XM, XC, QT, KT = W_[0:4], W_[4:8], W_[8:12], W_[12:16]
            for i in range(4):
                ps = ps_dense()
                project(l, "xm%d" % i, H, ps)
                P.copy(XM[i][:, 3:3 + T].r(), ps, eng="act")
            if stop == "xm":
                break
            gblk = wget(l, "gates")
            IMr, Cr, Ar, EBr, DTr = ROW[0], ROW[1], ROW[2], ROW[3], ROW[4]
            ps = ps_dense()
            project(l, None, H, ps[0:4, :], M=4, lcol0=0, blk=gblk)
            P.copy(IMr[0:4, :], ps[0:4, :], eng="act")
            ps = ps_dense()
            project(l, None, H, ps[0:4, :], M=4, lcol0=4, blk=gblk)
            P.act(Cr[0:4, :], ps[0:4, :], AF.Exp, bias=NBF[l][0:4, 0:1], scale=-1.0)
            P.act(Cr[0:4, :], Cr[0:4, :], AF.Ln, bias=1.0)
            ps = ps_dense()
            project(l, None, H, ps[0:8, :], M=8, lcol0=8, blk=gblk)
            P.act(DTr[0:8, :], ps[0:8, :], AF.Exp, bias=cvc(l, "dt_bias", rows=8))
            P.act(DTr[0:8, :], DTr[0:8, :], AF.Ln, bias=1.0)
            P.scan(Cr[0:4, :], RESET[0:4, :], Cr[0:4, :], 0.0, ALU.mult, ALU.add)
            P.tt(Ar[0:4, :], IMr[0:4, :], Cr[0:4, :], ALU.add)
            P.ts(Ar[0:4, :], Ar[0:4, :], cvc(l, "b_i", rows=4), ALU.add, -0.5 * float(np.log(128.0)), ALU.add)
            P.act(Ar[0:4, :], Ar[0:4, :], AF.Exp)
            P.act(EBr[0:4, :], Cr[0:4, :], AF.Exp, scale=-1.0)
            dump("mrows", ti, l, [Cr[0:4, :], Ar[0:4, :], EBr[0:4, :], DTr[0:8, :]])
            for c in range(NCH):
                pst = ps_small()
                P.transpose(pst[0:64, 0:4], Ar[0:4, c * LC:(c + 1) * LC], IDENT[0:4, 0:4])
                P.copy(GT[:, c * 32: c * 32 + 4], pst[0:64, 0:4])
            if stop == "gates":
                break
            cacc = FS[2][:, 0:T]
            for i in range(4):
                conv4(l, cacc, cacc, XM[i], "cmw", "cmb", i, 4, HMR[l][:, i * 3:i * 3 + 3], True)
                P.act(XC[i][:, 0:T].r(), cacc, AF.Silu)
            dump("xc", ti, l, [v[:, 0:T] for v in XC])
            if stop == "conv":
                break
            for h in range(4):
                ps = ps_dense()
                P.mm(ps, WSt[:, WS_Q + h * 128: WS_Q + (h + 1) * 128], XC[h][:, 0:T])
                P.copy(QT[h][:, 0:T].r(), ps, eng="act")
                ps = ps_dense()
                P.mm(ps, WSt[:, WS_K + h * 128: WS_K + (h + 1) * 128], XC[h][:, 0:T])
                P.copy(KT[h][:, 0:T].r(), ps, eng="act")
            if stop == "qk":
                break
            EBT = [[TMP[0], TMP[1]], [FS[7][:, 0:T], FS[8][:, 0:T]]]
            post_q = []
            for hp in range(2):
                heads = (2 * hp, 2 * hp + 1)
                NUM = {heads[0]: PL[0], heads[1]: PL[1]}
                DEN = {heads[0]: PL[2], heads[1]: PL[3]}
                EBS = {heads[0]: EBT[hp][0], heads[1]: EBT[hp][1]}
                for h in heads:
                    ps = ps_dense()
                    bcast_row(ps, EBr, h, 4, ROW[0])
                    P.copy(EBS[h], ps, eng="act")
                if stop == "ebs":
                    break
                for c in range(NCH):
                    cs = slice(c * LC, (c + 1) * LC)
                    if stop in ("c0", "c0a", "c0a1", "c0b", "c0c") and c == 1:
                        break
                    pkk = ps_dense()
                    pkv = ps_dense()
                    for q, h in enumerate(heads):
                        P.mm(pkk[0:64, q * 128:(q + 1) * 128], XC[h][:, cs], WSt[:, WS_K + h * 128: WS_K + (h + 1) * 128])
                    for q, h in enumerate(heads):
                        P.mm(pkv[0:64, q * 128:(q + 1) * 128], XM[h][:, 3 + c * LC: 3 + (c + 1) * LC],
                             WSt[:, WS_V + h * 128: WS_V + (h + 1) * 128])
                    psts = []
                    for q, h in enumerate(heads):
                        pst = ps_small()
                        P.mm(pst[0:64, 0:64], KT[h][:, cs], QT[h][:, cs])
                        psts.append(pst)
                    KA, VE = TK[(c % 2) * 2], TK[(c % 2) * 2 + 1]
                    P.tt(KA.rr("p (h d) -> p h d", h=2).r(), pkk[0:64, 0:256].rr("p (h d) -> p h d", h=2),
                         GT[:, c * 32 + heads[0]: c * 32 + heads[0] + 2].us(2).bc([64, 2, 128]), ALU.mult)
                    P.copy(VE.r(), pkv[0:64, 0:256], eng="act")
                    ats = []
                    for q, h in enumerate(heads):
                        at = AT_[(2 * c + q) % 4]
                        P.stt(at.r(), psts[q][0:64, 0:64], GT[:, c * 32 + h: c * 32 + h + 1], MASKT, ALU.mult, ALU.mult)
                        ats.append(at)
                    for q, h in enumerate(heads):
                        at = ats[q]
                        P.mm(NUM[h][:, cs], VE[:, q * 128:(q + 1) * 128], at, start=True, stop=False)
                        P.mm(NUM[h][:, cs], CS[l][h][:, 0:128], QT[h][:, cs], start=False, stop=True)
                        P.mm(DEN[h][:, cs], ONESR[0:64, :], at, start=True, stop=False)
                        P.mm(DEN[h][:, cs], CS[l][h][:, 128:256], QT[h][:, cs], start=False, stop=True)
                    for q, h in enumerate(heads):
                        pu = ps_small()
                        P.mm(pu[:, 0:128], KA[:, q * 128:(q + 1) * 128], VE[:, q * 128:(q + 1) * 128])
                        P.mm(pu[:, 128:256], KA[:, q * 128:(q + 1) * 128], ONESR[0:64, :])
                        P.tt(CSTMP[q], CS[l][h], pu[:, 0:256], ALU.add)
                        P.ts(CS[l][h].r(), CSTMP[q], EBS[h][:, c * LC + LC - 1: c * LC + LC], ALU.mult)
                    for _ in range(4):
                        if post_q:
                            post_q.pop(0)()
                while post_q:
                    post_q.pop(0)()
                if stop in ("c0", "cloop", "c0a", "c0a1", "c0b", "c0c"):
                    break
                for q, h in enumerate(heads):
                    if hp == 0:
                        nsrc, dsrc = FS[3 + 2 * q][:, 0:T], FS[4 + 2 * q][:, 0:T]
                        P.copy(nsrc, NUM[h], eng="act")
                        P.copy(dsrc, DEN[h], eng="act")
                    else:
                        nsrc, dsrc = NUM[h], DEN[h]

                    def mk(h=h, nsrc=nsrc, dsrc=dsrc, ebs=EBS[h]):
                        dn, t2, sg = TMP[2], FS[0][:, 0:T], FS[1][:, 0:T]
                        st = {}

                        def f_proj():
                            st["ps"] = ps_dense()
                            project(l, "om%d" % h, H, st["ps"])
                            P.act(sg, st["ps"], AF.Sigmoid)

                        def f_norm_mm():
                            P.act(SQ[0].r(), t2, AF.Square)
                            ps = ps_dense()
                            P.mm(ps, ONESR, SQ[0])
                            P.act(dn, ps, AF.Ln, bias=EPS, scale=1.0 / 128)
                        return [
                            f_proj,
                            lambda: P.tt(dn, dsrc, ebs, ALU.mult),
                            lambda: P.ts(t2, dn, -1.0, ALU.mult, 1.0, ALU.max),
                            lambda: P.tt(t2, t2, dn, ALU.max),
                            lambda: P.act(t2, t2, AF.Ln),
                            lambda: P.act(t2, t2, AF.Exp, scale=-1.0),
                            lambda: P.tt(t2, t2, ebs, ALU.mult),
                            lambda: P.tt(t2, nsrc, t2, ALU.mult),
                            lambda: P.tt(t2, t2, sg, ALU.mult),
                            f_norm_mm,
                            lambda: P.act(dn, dn, AF.Exp, scale=-0.5),
                            lambda: P.stt(YM[h][:, 0:T].r(), t2, cvc(l, "norm_m", h), dn, ALU.mult, ALU.mult),
                        ]
                    post_q.extend(mk())
            while post_q:
                post_q.pop(0)()
            dump("ym", ti, l, [y[:, 0:T] for y in YM])
            if stop in ("mlstm", "ebs", "c0", "cloop", "c0a", "c0a1", "c0b", "c0c"):
                break

            XR = FS[0:4]
            RT = [[FS[4 + 4 * b + k][:, 0:T] for k in range(4)] for b in range(2)]
            cacc = TMP[0]
            XCR = W_[0:4]
            for i in range(4):
                ps = ps_dense()
                project(l, "xr%d" % i, H, ps)
                P.copy(XR[i][:, 3:3 + T], ps, eng="act")

            def rg_a(j):
                Rt, It, At, St = RT[j % 2]
                conv4(l, cacc, XCR[j][:, 0:T].r(), XR[j], "crw", "crb", j, 4, HM[l][:, j * 3:j * 3 + 3], False)
                ps = ps_dense()
                P.mm(ps, WSt[:, WS_A + j * 128: WS_A + (j + 1) * 128], XCR[j][:, 0:T])
                P.act(Rt, ps, AF.Sigmoid, bias=cvc(l, "b_a", j))
                ps = ps_dense()
                P.mm(ps, WSt[:, WS_X + j * 128: WS_X + (j + 1) * 128], XCR[j][:, 0:T])
                P.act(It, ps, AF.Sigmoid, bias=cvc(l, "b_x", j))
                P.act(At, Rt, AF.Exp, scale=NCF[l][:, j:j + 1])
                P.act(St, At, AF.Square)
                P.act(St, St, AF.Sqrt, bias=1.0, scale=-1.0)

            def rg_b(j):
                Rt, It, At, St = RT[j % 2]
                P.tt(It, It, XCR[j][:, 0:T], ALU.mult)
                P.tt(It, It, St, ALU.mult)
                P.scan(Rt, At, It, HST[l][:, j:j + 1], ALU.mult, ALU.add)
                P.copy(HST[l][:, j:j + 1], Rt[:, T - 1:T])
                ps = ps_dense()
                project(l, "yr%d" % j, H, ps)
                P.act(St, ps, AF.Gelu_apprx_tanh)
                P.tt(YR[j][:, 0:T].r(), Rt, St, ALU.mult)
            rg_a(0)
            for j in range(4):
                if j + 1 < 4:
                    rg_a(j + 1)
                rg_b(j)
            dump("yr", ti, l, [y[:, 0:T] for y in YR])
            if stop == "rglru":
                break

            XBC = FS[0:8]
            cacc = FS[8][:, 0:T]
            XSC = W_[0:8]
            for i in range(8):
                ps = ps_dense()
                project(l, "xbc%d" % i, H, ps)
                P.copy(XBC[i][:, 3:3 + T], ps, eng="act")
            for i in range(8):
                conv4(l, cacc, cacc, XBC[i], "csw", "csb", i, 8, HM[l][:, 12 + i * 3: 12 + i * 3 + 3], False)
                P.act(XSC[i][:, 0:T].r(), cacc, AF.Silu)
            XS, BT, CT = [v[:, 0:T] for v in XSC[0:4]], [v[:, 0:T] for v in XSC[4:6]], [v[:, 0:T] for v in XSC[6:8]]
            ACN, Wr = ROW[0], ROW[1]
            P.ts(ACN[0:8, :], DTr[0:8, :], NBF[l][0:8, 1:2], ALU.mult)
            P.scan(ACN[0:8, :], RESET[0:8, :], ACN[0:8, :], 0.0, ALU.mult, ALU.add)
            P.tt(Wr[0:8, :].rr("p (c t) -> p c t", t=LC),
                 ACN[0:8, :].rr("p (c t) -> p c t", t=LC)[:, :, LC - 1:LC].bc([8, NCH, LC]),
                 ACN[0:8, :].rr("p (c t) -> p c t", t=LC), ALU.subtract)
            P.act(Wr[0:8, :], Wr[0:8, :], AF.Exp, scale=-1.0)
            P.tt(Wr[0:8, :], Wr[0:8, :], DTr[0:8, :], ALU.mult)
            dump("srows", ti, l, [ACN[0:8, :], Wr[0:8, :], DTr[0:8, :]])
            for c in range(NCH):
                pst = ps_small()
                cs = slice(c * LC, (c + 1) * LC)
                P.transpose(pst[0:64, 0:8], ACN[0:8, cs], IDENT[0:8, 0:8])
                P.transpose(pst[0:64, 8:16], DTr[0:8, cs], IDENT[0:8, 0:8])
                P.transpose(pst[0:64, 16:24], Wr[0:8, cs], IDENT[0:8, 0:8])
                P.copy(GT[:, c * 32 + 4: c * 32 + 28], pst[0:64, 0:24])
            GT3 = GT.rr("p (c k) -> p c k", k=32)
            GY = [FS[i][:, 0:T] for i in range(4)]
            for g in range(2):
                pcb = ps_dense()
                for c in range(NCH):
                    cs = slice(c * LC, (c + 1) * LC)
                    P.mm(pcb[0:64, cs], BT[g][:, cs], CT[g][:, cs])
                P.tt(CBM.rr("p (c t) -> p c t", t=LC), pcb[0:64, :].rr("p (c t) -> p c t", t=LC),
                     MASKT.us(1).bc([64, NCH, LC]), ALU.mult)
                EAC2 = [TMP[0], TMP[1]]
                EATOT = SMALL
                for q in range(4):
                    h = 4 * g + q
                    pa = ps_dense()
                    bcast_row(pa, ACN, h, 8, ROW[2])
                    half = slice(64 * (q % 2), 64 * (q % 2) + 64)
                    P.act(EAC2[q // 2][half, :], pa[half, :], AF.Exp, scale=-1.0)
                    P.act(EATOT[:, q * 8:(q + 1) * 8], pa.rr("p (c t) -> p c t", t=LC)[:, :, LC - 1], AF.Exp, scale=-1.0)
                    P.tt(SEGT.rr("p (c t) -> p c t", t=LC), GT3[:, :, 4 + h:5 + h].bc([64, NCH, LC]),
                         pa[0:64, :].rr("p (c t) -> p c t", t=LC), ALU.subtract)
                    P.ts(SEGT, SEGT, 0.0, ALU.min)
                    P.act(SEGT, SEGT, AF.Exp)
                    P.tt(SEGT.rr("p (c t) -> p c t", t=LC), SEGT.rr("p (c t) -> p c t", t=LC),
                         GT3[:, :, 12 + h:13 + h].bc([64, NCH, LC]), ALU.mult)
                    P.tt(MT[q].r(), SEGT, CBM, ALU.mult)
                YD = [PL[0], PL[1]]
                YO = [PL[2], PL[3]]
                for c in range(NCH):
                    cs = slice(c * LC, (c + 1) * LC)
                    pxt = ps_small()
                    for jj in range(2):
                        P.transpose(pxt[0:64, jj * 128:(jj + 1) * 128], XS[2 * g + jj][:, cs], IDENT)
                    P.transpose(pxt[0:64, 256:384], BT[g][:, cs], IDENT)
                    XT_, XW, BTK = TK[(c % 2) * 3], TK[(c % 2) * 3 + 1], TK[(c % 2) * 3 + 2]
                    P.copy(XT_.r(), pxt[0:64, 0:256], eng="act")
                    P.tt(XW.rr("p (h d) -> p h d", h=4).r(), pxt[0:64, 0:256].rr("p (h d) -> p h d", h=4),
                         GT[:, c * 32 + 20 + 4 * g: c * 32 + 24 + 4 * g].us(2).bc([64, 4, 64]), ALU.mult)
                    P.copy(BTK[:, 0:128].r(), pxt[0:64, 256:384], eng="act")
                    for jj in range(2):
                        j = 2 * g + jj
                        for q2 in range(2):
                            q = 2 * jj + q2
                            P.mm(YD[jj][64 * q2:64 * q2 + 64, cs], XT_[:, q * 64:(q + 1) * 64], MT[q][:, cs], r32=False)
                        P.mm(YO[jj][:, cs], SST[l][j], CT[g][:, cs])
                    pu = ps_small()
                    P.mm(pu[:, 0:256], BTK[:, 0:128], XW)
                    for jj in range(2):
                        j = 2 * g + jj
                        for q2 in range(2):
                            q = 2 * jj + q2
                            P.stt(SST[l][j][:, q2 * 64:(q2 + 1) * 64].r(), SST[l][j][:, q2 * 64:(q2 + 1) * 64],
                                  EATOT[:, q * 8 + c: q * 8 + c + 1], pu[:, q * 64:(q + 1) * 64], ALU.mult, ALU.add)
                for jj in range(2):
                    j = 2 * g + jj
                    y, sz = TMP[2], FS[9][:, 0:T]
                    ps = ps_dense()
                    project(l, "zs%d" % j, H, ps)
                    P.act(sz, ps, AF.Silu)
                    P.tt(y, YO[jj], EAC2[jj], ALU.mult)
                    P.tt(y, y, YD[jj], ALU.add)
                    P.stt(y, XS[j], cvc(l, "dskip", j), y, ALU.mult, ALU.add)
                    P.tt(GY[j], y, sz, ALU.mult)
            rs = TMP[2]
            rmsnorm_stats(GY, 4, 512, rs)
            for j in range(4):
                P.stt(YS[j][:, 0:T].r(), GY[j], cvc(l, "norm_s", j), rs, ALU.mult, ALU.mult)
            dump("ys", ti, l, [y[:, 0:T] for y in YS])
            if stop == "ssd":
                break

            YALL = [y[:, 0:T] for y in (YM + YR + YS)]
            MIX = [s[:, 0:T] for s in FS[0:8]]
            for n in range(8):
                ps = ps_dense()
                project(l, "out%da" % n, YALL[0:8], ps, last=False)
                project(l, "out%db" % n, YALL[8:12], ps, first=False)
                P.copy(MIX[n], ps, eng="act")
            rs = TMP[2]
            rmsnorm_stats(MIX, 8, D, rs)
            for c in range(8):
                P.stt(MIX[c], MIX[c], cvc(l, "g_mix_post", c), rs, ALU.mult, ALU.mult)
                P.tt(X[c], X[c], MIX[c], ALU.add)
            dump("x1", ti, l, X)
            if stop == "mix":
                break

            rs = TMP[2]
            rmsnorm_stats(X, 8, D, rs)
            for c in range(8):
                P.stt(H[c].r(), X[c], cvc(l, "g_ffn_pre", c), rs, ALU.mult, ALU.mult)
            ACTF = [s[:, 0:T] for s in RS_[0:NJ]]
            UB = FS[0:4]
            DST = [[TMP[0], TMP[1]], [FS[4][:, 0:T], FS[5][:, 0:T]]]
            dense_banks[0] = [0, 1, 4, 5, 6, 7]
            for j in range(NJ + 1):
                if j < NJ:
                    info = []
                    for q, nm in enumerate(("upg", "upv")):
                        ps = ps_dense()
                        project(l, "%s%d" % (nm, j), H, ps)
                        ub = UB[(j % 2) * 2 + q]
                        cidx = q * NJ + j
                        wv = [CVt[l][:, CV["cfw"] + k * 44 + cidx: CV["cfw"] + k * 44 + cidx + 1] for k in range(3)]
                        b = CVt[l][:, CV["cfb"] + cidx: CV["cfb"] + cidx + 1]
                        dst = DST[j % 2][q]
                        P.act(dst, ps, AF.Identity, bias=b, scale=wv[2])
                        P.copy(ub[:, 3:3 + T], ps, eng="act")
                        P.copy(ub[:, 1:3], HF[l][:, cidx * 2: cidx * 2 + 2], eng=HALO_ENG)
                        info.append((ub, dst, wv, cidx))
                    for k in (1, 0):
                        for ub, dst, wv, cidx in info:
                            P.stt(dst, ub[:, 1 + k:1 + k + T], wv[k], dst, ALU.mult, ALU.add)
                    for ub, dst, wv, cidx in info:
                        P.copy(HF[l][:, cidx * 2: cidx * 2 + 2], ub[:, T + 1:T + 3], eng=HALO_ENG)
                if j > 0:
                    rg, rv = DST[(j - 1) % 2]
                    P.act(rg, rg, AF.Gelu_apprx_tanh)
                    P.tt(ACTF[j - 1].r(), rg, rv, ALU.mult)
            FO = [s[:, 0:T] for s in FS[4:12]]
            for n in range(8):
                ps = ps_dense()
                project(l, "dn%da" % n, ACTF[0:8], ps, last=False)
                project(l, "dn%db" % n, ACTF[8:16], ps, first=False, last=False)
                project(l, "dn%dc" % n, ACTF[16:22], ps, first=False)
                P.copy(FO[n], ps, eng="act")
            rs = TMP[2]
            rmsnorm_stats(FO, 8, D, rs)
            for c in range(8):
                P.stt(FO[c], FO[c], cvc(l, "g_ffn_post", c), rs, ALU.mult, ALU.mult)
                P.tt(X[c], X[c], FO[c], ALU.add)
            dump("x2", ti, l, X)
        else:
            for c in range(8):
                final_evs.append(P.dma(outT[c * 128:(c + 1) * 128, t0:t0 + T], X[c]))
            continue
        break
    P.emit(final_events=[e for e in final_evs if e is not None])
    return nc, P


_CACHE = {}


def kernel(**inputs):
    x = np.asarray(inputs["x"], np.float32)
    B = x.shape[0]
    wbig, wsm, cvec = host_layout(inputs)
    cmat = host_consts()
    if "nc" not in _CACHE:
        _CACHE["nc"] = build()[0]
    nc = _CACHE["nc"]
    in_maps = []
    for b in range(B):
        m = {"xT": np.ascontiguousarray(x[b].T), "cvec": cvec, "cmat": cmat}
        for l in range(DEPTH):
            m["wbig%d" % l] = wbig[l]
            m["wsm%d" % l] = wsm[l]
        in_maps.append(m)
    res = run_bass_kernel_spmd(nc, in_maps, core_ids=list(range(B)))
    out = np.stack([np.ascontiguousarray(r["outT"].T) for r in res.results], axis=0)
    return out.astype(np.float32)
```

```python
import contextlib
import numpy as np
import concourse.bass as bass
import concourse.mybir as mybir
from concourse.bass_utils import run_bass_kernel_spmd

F32 = mybir.dt.float32
F32R = mybir.dt.float32r
AF = mybir.ActivationFunctionType
ALU = mybir.AluOpType

SAME_ENGINE_SYNC = True
HALO_ENG = "dve"


class Buf:
    __slots__ = ("name", "lastw", "readers", "dsem", "dcount", "psum")

    def __init__(self, name, psum=False):
        self.name = name
        self.psum = psum
        self.lastw = None
        self.readers = []
        self.dsem = None
        self.dcount = 0


class V:
    __slots__ = ("buf", "ap")

    def __init__(self, buf, ap):
        self.buf = buf
        self.ap = ap

    def __getitem__(self, k):
        return V(self.buf, self.ap[k])

    def rr(self, s, **kw):
        return V(self.buf, self.ap.rearrange(s, **kw))

    def bc(self, shape):
        return V(self.buf, self.ap.to_broadcast(list(shape)))

    def us(self, axis):
        return V(self.buf, self.ap.unsqueeze(axis))

    def r(self):
        return V(self.buf, self.ap.bitcast(F32R))

    def f(self):
        return V(self.buf, self.ap.bitcast(F32))

    @property
    def shape(self):
        return self.ap.shape


def _ap(x):
    return x.ap if isinstance(x, V) else x


class Prog:
    ENGS = ("pe", "act", "dve", "pool", "sp")

    def __init__(self, nc):
        self.nc = nc
        self.stack = contextlib.ExitStack()
        self.ops = {e: [] for e in self.ENGS}
        self.sems = {}
        self.ecount = {e: 0 for e in self.ENGS}
        self.seen = {e: {} for e in self.ENGS}
        for e in ("pe", "act", "dve", "pool"):
            self.sems[e] = self.stack.enter_context(nc.semaphore("e_" + e))
        self.ctotal = {}
        for q in ("sp", "pool"):
            self.sems["const_" + q] = self.stack.enter_context(nc.semaphore("consts_" + q))
            self.ctotal[q] = 0
        self.n_inst = 0

    def sbuf(self, name, shape, dtype=F32):
        t = self.stack.enter_context(self.nc.sbuf_tensor(name, list(shape), dtype))
        return V(Buf(name), t[:])

    def psum(self, name, shape, dtype=F32):
        t = self.stack.enter_context(self.nc.psum_tensor(name, list(shape), dtype))
        return V(Buf(name, psum=True), t[:])

    def _collect(self, eng, reads, writes):
        need = {}

        def add(ev):
            if ev is None:
                return
            k, val = ev
            if need.get(k, 0) < val:
                need[k] = val
        for b in reads:
            add(b.lastw)
            if b.psum:
                for ev in b.readers:
                    if ev[0] != eng:
                        add(ev)
        for b in writes:
            add(b.lastw)
            for ev in b.readers:
                add(ev)
        waits = []
        for k, val in need.items():
            if k == eng and (eng == "pe" or not SAME_ENGINE_SYNC):
                continue
            if self.seen[eng].get(k, 0) >= val:
                continue
            self.seen[eng][k] = val
            waits.append((k, val))
        return waits

    def _commit(self, ev, reads, writes):
        for b in writes:
            b.lastw = ev
            b.readers = []
        for b in reads:
            if b in writes:
                continue
            b.readers = [r for r in b.readers if r[0] != ev[0]] + [ev]

    def op(self, eng, fn, reads, writes):
        reads = list({id(v.buf): v.buf for v in reads if isinstance(v, V)}.values())
        writes = list({id(v.buf): v.buf for v in writes if isinstance(v, V)}.values())
        waits = self._collect(eng, reads, writes)
        self.ecount[eng] += 1
        ev = (eng, self.ecount[eng])
        self.ops[eng].append((waits, fn, (eng, 1)))
        self._commit(ev, reads, writes)
        self.n_inst += 1

    def dma(self, out, in_, queue="sp", const=False):
        reads = [in_.buf] if isinstance(in_, V) else []
        writes = [out.buf] if isinstance(out, V) else []
        waits = self._collect(queue, reads, writes)
        o, i = _ap(out), _ap(in_)
        fn = lambda e, o=o, i=i: e.dma_start(out=o, in_=i)
        if const:
            self.ctotal[queue] += 16
            self.ops[queue].append((waits, fn, ("const_" + queue, 16), True))
            return None
        b = (writes + reads)[0]
        if b.dsem is None:
            key = "d%d" % len(self.sems)
            self.sems[key] = self.stack.enter_context(self.nc.semaphore(key))
            b.dsem = key
        b.dcount += 16
        ev = (b.dsem, b.dcount)
        self.ops[queue].append((waits, fn, (b.dsem, 16)))
        self._commit(ev, reads, writes)
        self.n_inst += 1
        return ev

    def mm(self, out, lhsT, rhs, start=True, stop=True, r32=True):
        l = lhsT.ap.bitcast(F32R) if r32 else lhsT.ap
        r = rhs.ap.bitcast(F32R) if r32 else rhs.ap
        o = out.ap
        self.op("pe", lambda e: e.matmul(o, l, r, start=start, stop=stop), [lhsT, rhs], [out])

    def transpose(self, out, in_, ident):
        o, i, d = out.ap, in_.ap, ident.ap
        self.op("pe", lambda e: e.transpose(o, i, d), [in_, ident], [out])

    def act(self, out, in_, func, bias=0.0, scale=1.0):
        o, i = out.ap, in_.ap
        b, s = _ap(bias), _ap(scale)
        rd = [in_] + [x for x in (bias, scale) if isinstance(x, V)]
        self.op("act", lambda e: e.activation(out=o, in_=i, func=func, bias=b, scale=s), rd, [out])

    def tt(self, out, in0, in1, op, eng="dve"):
        o, a, b = out.ap, in0.ap, in1.ap
        self.op(eng, lambda e: e.tensor_tensor(out=o, in0=a, in1=b, op=op), [in0, in1], [out])

    def ts(self, out, in0, s1, op0, s2=None, op1=None, eng="dve"):
        o, a = out.ap, in0.ap
        x1, x2 = _ap(s1), _ap(s2)
        rd = [in0] + [x for x in (s1, s2) if isinstance(x, V)]
        if op1 is None:
            self.op(eng, lambda e: e.tensor_scalar(out=o, in0=a, scalar1=x1, scalar2=None, op0=op0), rd, [out])
        else:
            self.op(eng, lambda e: e.tensor_scalar(out=o, in0=a, scalar1=x1, scalar2=x2, op0=op0, op1=op1),
                    rd, [out])

    def stt(self, out, in0, scalar, in1, op0, op1):
        o, a, b = out.ap, in0.ap, in1.ap
        s = _ap(scalar)
        rd = [in0, in1] + ([scalar] if isinstance(scalar, V) else [])
        self.op("dve", lambda e: e.scalar_tensor_tensor(out=o, in0=a, scalar=s, in1=b, op0=op0, op1=op1),
                rd, [out])

    def copy(self, out, in_, eng="dve"):
        o, i = out.ap, in_.ap
        if eng == "act":
            self.op("act", lambda e: e.activation(out=o, in_=i, func=AF.Copy), [in_], [out])
        else:
            self.op(eng, lambda e: e.tensor_copy(out=o, in_=i), [in_], [out])

    def recip(self, out, in_):
        o, i = out.ap, in_.ap
        self.op("dve", lambda e: e.reciprocal(out=o, in_=i), [in_], [out])

    def scan(self, out, d0, d1, initial, op0, op1):
        o, a, b = out.ap, d0.ap, d1.ap
        ini = _ap(initial)
        rd = [d0, d1] + ([initial] if isinstance(initial, V) else [])
        self.op("dve", lambda e: e.tensor_tensor_scan(out=o, data0=a, data1=b, initial=ini, op0=op0, op1=op1),
                rd, [out])

    def memset(self, out, val, eng="dve"):
        o = out.ap
        self.op(eng, lambda e: e.memset(o, val), [], [out])

    def emit(self, final_events=()):
        nc = self.nc
        sems = self.sems
        ops = self.ops
        ctotal = self.ctotal

        def run(e, name):
            first = True
            for rec in ops[name]:
                waits, fn, inc = rec[:3]
                if first and len(rec) == 3:
                    for q, tot in ctotal.items():
                        if tot > 0:
                            e.wait_ge(sems["const_" + q], tot)
                    first = False
                for k, val in waits:
                    e.wait_ge(sems[k], val)
                fn(e).then_inc(sems[inc[0]], inc[1])
            if name == "sp":
                for k, val in final_events:
                    e.wait_ge(sems[k], val)

        with nc.Block() as block:
            @block.sync
            def _(e):
                run(e, "sp")

            @block.tensor
            def _(e):
                run(e, "pe")

            @block.scalar
            def _(e):
                run(e, "act")

            @block.vector
            def _(e):
                run(e, "dve")

            @block.gpsimd
            def _(e):
                run(e, "pool")
        self.stack.close()


D = 1024
S = 4096
DEPTH = 2
T = 512
NT = S // T
LC = 64
NCH = T // LC
DFF = 2816
NJ = DFF // 128
EPS = 1e-6
HALO = 3
SW = T + HALO

C_XM, C_OM, C_IM, C_FM, C_XR, C_YR, C_ZS, C_XBC, C_DT = 0, 512, 1024, 1028, 1032, 1544, 2056, 2568, 3592

WB = {}
_off = 0


def _wb(name, kc, ncols=128):
    global _off
    WB[name] = (_off, kc, ncols)
    _off += kc * ncols


for _i in range(4):
    _wb("xm%d" % _i, 8)
for _i in range(4):
    _wb("om%d" % _i, 8)
_wb("gates", 8, 16)
for _i in range(4):
    _wb("xr%d" % _i, 8)
for _i in range(4):
    _wb("yr%d" % _i, 8)
for _i in range(4):
    _wb("zs%d" % _i, 8)
for _i in range(8):
    _wb("xbc%d" % _i, 8)
for _n in range(8):
    _wb("out%da" % _n, 8)
    _wb("out%db" % _n, 4)
for _j in range(NJ):
    _wb("upg%d" % _j, 8)
    _wb("upv%d" % _j, 8)
for _n in range(8):
    _wb("dn%da" % _n, 8)
    _wb("dn%db" % _n, 8)
    _wb("dn%dc" % _n, 6)
WCOLS = _off
WSEQ = (["xm%d" % i for i in range(4)] + ["gates"] + ["om%d" % i for i in range(4)]
        + ["xr%d" % i for i in range(4)] + ["yr%d" % i for i in range(4)]
        + ["xbc%d" % i for i in range(8)] + ["zs%d" % i for i in range(4)]
        + [n for n in WB if n.startswith("out")] + [n for n in WB if n.startswith("up")]
        + [n for n in WB if n.startswith("dn")])
assert len(WSEQ) == len(WB)

WS_Q, WS_K, WS_V, WS_A, WS_X = 0, 512, 1024, 1536, 2048
WSCOLS = 2560

CV = {}
_cvo = 0


def _cv(name, n):
    global _cvo
    CV[name] = _cvo
    _cvo += n


for _n, _k in [("g_mix_pre", 8), ("g_mix_post", 8), ("g_ffn_pre", 8), ("g_ffn_post", 8),
               ("cmw", 16), ("cmb", 4), ("norm_m", 4),
               ("crw", 16), ("crb", 4), ("b_a", 4), ("b_x", 4), ("lam", 4),
               ("csw", 32), ("csb", 8), ("norm_s", 4), ("dskip", 4),
               ("cfw", 3 * 44), ("cfb", 44),
               ("b_i", 1), ("b_f", 1), ("dt_bias", 1), ("a_log", 1)]:
    _cv(_n, _k)
NCV = _cvo

CM_ID, CM_ONES, CM_MASK, CM_RESET, CM_SEL = 0, 128, 256, 320, 832
NCM = 832


def host_consts():
    cm = np.zeros((128, NCM), np.float32)
    cm[:, CM_ID:CM_ID + 128] = np.eye(128, dtype=np.float32)
    cm[:, CM_ONES:CM_ONES + 128] = 1.0
    s = np.arange(64)[:, None]
    t = np.arange(64)[None, :]
    cm[:64, CM_MASK:CM_MASK + 64] = (s <= t).astype(np.float32)
    r = np.ones(512, np.float32)
    r[::64] = 0.0
    cm[:, CM_RESET:CM_RESET + 512] = r[None, :]
    return cm


def _pc(v, n):
    return np.ascontiguousarray(np.asarray(v, np.float32).reshape(n, 128).T)


def _blk(W, col0, kc, ncols=128, row0=0):
    sub = W[row0:row0 + kc * 128, col0:col0 + ncols]
    return np.ascontiguousarray(sub.reshape(kc, 128, ncols).transpose(1, 0, 2).reshape(128, kc * ncols))


def host_layout(inp):
    wbig, wsm = [], []
    cvec = np.zeros((DEPTH, 128, NCV), np.float32)
    for l in range(DEPTH):
        w_in, w_out, w_up, w_dn = (np.asarray(inp[k][l], np.float32) for k in ("w_in", "w_out", "w_up", "w_down"))
        big = np.zeros((128, WCOLS), np.float32)

        def put(name, arr):
            o, kc, ncols = WB[name]
            big[:, o:o + kc * ncols] = arr
        for i in range(4):
            put("xm%d" % i, _blk(w_in, C_XM + i * 128, 8))
            put("om%d" % i, _blk(w_in, C_OM + i * 128, 8))
            put("xr%d" % i, _blk(w_in, C_XR + i * 128, 8))
            put("yr%d" % i, _blk(w_in, C_YR + i * 128, 8))
            put("zs%d" % i, _blk(w_in, C_ZS + i * 128, 8))
        for i in range(8):
            put("xbc%d" % i, _blk(w_in, C_XBC + i * 128, 8))
        gcols = np.concatenate([w_in[:, C_IM:C_IM + 4], w_in[:, C_FM:C_FM + 4], w_in[:, C_DT:C_DT + 8]], axis=1)
        put("gates", _blk(gcols, 0, 8, 16))
        for n in range(8):
            put("out%da" % n, _blk(w_out, n * 128, 8, row0=0))
            put("out%db" % n, _blk(w_out, n * 128, 4, row0=1024))
            put("dn%da" % n, _blk(w_dn, n * 128, 8, row0=0))
            put("dn%db" % n, _blk(w_dn, n * 128, 8, row0=1024))
            put("dn%dc" % n, _blk(w_dn, n * 128, 6, row0=2048))
        for j in range(NJ):
            put("upg%d" % j, _blk(w_up, j * 128, 8))
            put("upv%d" % j, _blk(w_up, DFF + j * 128, 8))
        wbig.append(big)
        sm = np.zeros((128, WSCOLS), np.float32)
        for h in range(4):
            sm[:, WS_Q + h * 128: WS_Q + (h + 1) * 128] = inp["w_q_m"][l][h]
            sm[:, WS_K + h * 128: WS_K + (h + 1) * 128] = inp["w_k_m"][l][h]
            sm[:, WS_V + h * 128: WS_V + (h + 1) * 128] = inp["w_v_m"][l][h]
        for j in range(4):
            for q in range(2):
                sm[q * 64:(q + 1) * 64, WS_A + j * 128 + q * 64: WS_A + j * 128 + (q + 1) * 64] = inp["w_a_r"][l][2 * j + q]
                sm[q * 64:(q + 1) * 64, WS_X + j * 128 + q * 64: WS_X + j * 128 + (q + 1) * 64] = inp["w_x_r"][l][2 * j + q]
        wsm.append(sm)
        cv = cvec[l]
        for nm, key in [("g_mix_pre", "norm_mix_pre"), ("g_mix_post", "norm_mix_post"),
                        ("g_ffn_pre", "norm_ffn_pre"), ("g_ffn_post", "norm_ffn_post")]:
            cv[:, CV[nm]:CV[nm] + 8] = _pc(inp[key][l], 8)
        for k in range(4):
            cv[:, CV["cmw"] + k * 4: CV["cmw"] + k * 4 + 4] = _pc(inp["conv_m_w"][l][k], 4)
            cv[:, CV["crw"] + k * 4: CV["crw"] + k * 4 + 4] = _pc(inp["conv_r_w"][l][k], 4)
            cv[:, CV["csw"] + k * 8: CV["csw"] + k * 8 + 8] = _pc(inp["conv_s_w"][l][k], 8)
        for k in range(3):
            cv[:, CV["cfw"] + k * 44: CV["cfw"] + k * 44 + 44] = _pc(inp["conv_f_w"][l][k], 44)
        cv[:, CV["cfb"]:CV["cfb"] + 44] = _pc(inp["conv_f_b"][l], 44)
        cv[:, CV["cmb"]:CV["cmb"] + 4] = _pc(inp["conv_m_b"][l], 4)
        cv[:, CV["norm_m"]:CV["norm_m"] + 4] = _pc(inp["norm_m"][l], 4)
        cv[:, CV["crb"]:CV["crb"] + 4] = _pc(inp["conv_r_b"][l], 4)
        cv[:, CV["b_a"]:CV["b_a"] + 4] = _pc(inp["b_a_r"][l], 4)
        cv[:, CV["b_x"]:CV["b_x"] + 4] = _pc(inp["b_x_r"][l], 4)
        cv[:, CV["lam"]:CV["lam"] + 4] = _pc(inp["lam_r"][l], 4)
        cv[:, CV["csb"]:CV["csb"] + 8] = _pc(inp["conv_s_b"][l], 8)
        cv[:, CV["norm_s"]:CV["norm_s"] + 4] = _pc(inp["norm_s"][l], 4)
        cv[:, CV["dskip"]:CV["dskip"] + 4] = _pc(np.repeat(np.asarray(inp["d_skip_s"][l], np.float32), 64), 4)
        cv[0:4, CV["b_i"]] = inp["b_i_m"][l]
        cv[0:4, CV["b_f"]] = inp["b_f_m"][l]
        cv[0:8, CV["dt_bias"]] = inp["dt_bias_s"][l]
        cv[0:8, CV["a_log"]] = inp["a_log_s"][l]
    return wbig, wsm, cvec


def build(n_tiles=NT, depth=DEPTH, dbg=None, stop=None):
    nc = bass.Bass("TRN2", target_bir_lowering=False)
    xT = nc.dram_tensor("xT", [D, S], F32, kind="ExternalInput").ap()
    outT = nc.dram_tensor("outT", [D, S], F32, kind="ExternalOutput").ap()
    wbig_d = [nc.dram_tensor("wbig%d" % l, [128, WCOLS], F32, kind="ExternalInput").ap() for l in range(DEPTH)]
    wsm_d = [nc.dram_tensor("wsm%d" % l, [128, WSCOLS], F32, kind="ExternalInput").ap() for l in range(DEPTH)]
    cvec_d = nc.dram_tensor("cvec", [DEPTH, 128, NCV], F32, kind="ExternalInput").ap()
    cmat_d = nc.dram_tensor("cmat", [128, NCM], F32, kind="ExternalInput").ap()
    dbg = dbg or {}
    dbg_d = {}
    for name, (dt_, dl_, ncols) in dbg.items():
        dbg_d[name] = nc.dram_tensor("dbg_" + name, [128, ncols], F32, kind="ExternalOutput").ap()

    P = Prog(nc)
    final_evs = []

    CM = P.sbuf("CM", [128, NCM])
    P.dma(CM, cmat_d, const=True)
    ONESR = P.sbuf("ONESR", [128, 128], F32R)
    P.dma(ONESR, cmat_d[:, CM_ONES:CM_ONES + 128], queue="pool", const=True)
    IDENT = CM[:, CM_ID:CM_ID + 128]
    MASKT = CM[0:64, CM_MASK:CM_MASK + 64]
    RESET = CM[:, CM_RESET:CM_RESET + 512]

    def bcast_row(ps, rows, h, k, tmp):
        P.act(tmp[0:k, :], rows[0:k, :], AF.Copy, scale=IDENT[0:k, h:h + 1])
        P.mm(ps, CM[0:k, CM_ONES:CM_ONES + 128], tmp[0:k, :], r32=False)
    CVt = []
    for l in range(depth):
        cv = P.sbuf("CV%d" % l, [128, NCV])
        P.dma(cv, cvec_d[l], const=True)
        CVt.append(cv)
    WSt = P.sbuf("WS", [128, WSCOLS], F32R)

    def cvc(l, name, c=0, rows=128):
        o = CV[name] + c
        return CVt[l][0:rows, o:o + 1]

    X = [P.sbuf("X%d" % c, [128, T]) for c in range(8)]
    H = [P.sbuf("H%d" % c, [128, T]) for c in range(8)]
    RS_ = [P.sbuf("RS%d" % i, [128, SW]) for i in range(28)]
    FS = [P.sbuf("FS%d" % i, [128, SW]) for i in range(12)]
    NRING = 5
    RING = [P.sbuf("WR%d" % i, [128, 1024], F32R) for i in range(NRING)]
    TMP = [P.sbuf("TMP%d" % i, [128, T]) for i in range(3)]
    SQ = [P.sbuf("SQ%d" % i, [128, T]) for i in range(2)]
    ROW = [P.sbuf("ROW%d" % i, [8, T]) for i in range(5)]
    GT = P.sbuf("GT", [64, NCH * 32])
    TK = [P.sbuf("TK%d" % i, [64, 256]) for i in range(6)]
    AT_ = [P.sbuf("AT%d" % i, [64, 64]) for i in range(4)]
    MT = [P.sbuf("MT%d" % i, [64, T]) for i in range(4)]
    SEGT = P.sbuf("SEGT", [64, T])
    CBM = P.sbuf("CBM", [64, T])
    CSTMP = [P.sbuf("CSTMP%d" % i, [128, 256]) for i in range(2)]
    HF = [P.sbuf("HF%d" % l, [128, 44 * 2]) for l in range(depth)]
    HMR = [P.sbuf("HMR%d" % l, [128, 4 * 3]) for l in range(depth)]
    HM = [P.sbuf("HM%d" % l, [128, 12 * 3]) for l in range(depth)]
    CS = [[P.sbuf("CS%d_%d" % (l, h), [128, 256]) for h in range(4)] for l in range(depth)]
    SST = [[P.sbuf("SST%d_%d" % (l, j), [128, 128]) for j in range(4)] for l in range(depth)]
    HST = [P.sbuf("HST%d" % l, [128, 4]) for l in range(depth)]
    NCF = [P.sbuf("NCF%d" % l, [128, 4]) for l in range(depth)]
    NBF = [P.sbuf("NBF%d" % l, [8, 2]) for l in range(depth)]
    SMALL = P.sbuf("SMALL", [128, 64])

    PS = [P.psum("PS%d" % i, [128, T]) for i in range(8)]
    dense_rr = [0]
    small_rr = [0]

    dense_banks = [[0, 1]]

    def ps_dense():
        dense_rr[0] += 1
        lst = dense_banks[0]
        return PS[lst[dense_rr[0] % len(lst)]]

    def ps_small():
        small_rr[0] ^= 1
        return PS[2 + small_rr[0]]
    PL = PS[4:8]

    for l in range(depth):
        P.memset(HF[l], 0.0)
        P.memset(HM[l], 0.0)
        P.ts(HMR[l].r(), CM[:, 0:12], 0.0, ALU.mult)
        P.memset(HST[l], 0.0)
        for h in range(4):
            P.ts(CS[l][h].r(), CM[:, 0:256], 0.0, ALU.mult)
            P.ts(SST[l][h].r(), CM[:, 0:128], 0.0, ALU.mult)
        P.act(SMALL[:, 0:4], CVt[l][:, CV["lam"]:CV["lam"] + 4], AF.Exp, scale=-1.0)
        P.act(SMALL[:, 4:8], SMALL[:, 0:4], AF.Ln, bias=1.0)
        P.ts(NCF[l], SMALL[:, 4:8], -8.0, ALU.mult)
        P.ts(NBF[l][0:4, 0:1], cvc(l, "b_f", rows=4), -1.0, ALU.mult)
        P.act(NBF[l][0:8, 1:2], cvc(l, "a_log", rows=8), AF.Exp)

    total_blocks = n_tiles * depth * len(WSEQ)
    wstate = {"issued": 0, "used": 0}

    def _issue(k):
        tl, bi = divmod(k, len(WSEQ))
        l = tl % depth
        o, kc, ncols = WB[WSEQ[bi]]
        slot = RING[k % NRING]
        P.dma(slot[:, 0:kc * ncols], wbig_d[l][:, o:o + kc * ncols], queue="pool")

    def wget(l, name):
        k = wstate["used"]
        tl, bi = divmod(k, len(WSEQ))
        assert WSEQ[bi] == name and tl % depth == l, (WSEQ[bi], name, tl, l)
        while wstate["issued"] < min(total_blocks, k + NRING):
            _issue(wstate["issued"])
            wstate["issued"] += 1
        wstate["used"] += 1
        o, kc, ncols = WB[name]
        return RING[k % NRING], kc, ncols

    def dump(name, ti, l, views):
        if name in dbg and dbg[name][0] == ti and dbg[name][1] == l:
            for i, v in enumerate(views):
                n = v.shape[1]
                final_evs.append(P.dma(dbg_d[name][0:v.shape[0], i * n:(i + 1) * n], v))

    def rmsnorm_stats(src, nchunks, dim, rs_out):
        ps = ps_dense()
        for c in range(nchunks):
            sq = SQ[c % 2]
            if c % 2 == 0:
                P.act(sq.r(), src[c], AF.Square)
            else:
                P.tt(sq.r(), src[c], src[c], ALU.mult)
            P.mm(ps, ONESR, sq, start=(c == 0), stop=(c == nchunks - 1))
        P.act(rs_out, ps, AF.Ln, bias=EPS, scale=1.0 / dim)
        P.act(rs_out, rs_out, AF.Exp, scale=-0.5)

    def project(l, name, rhs_list, out_ps, M=128, lcol0=0, first=True, last=True, blk=None):
        slot, kc, ncols = blk if blk is not None else wget(l, name)
        for c in range(kc):
            P.mm(out_ps, slot[:, c * ncols + lcol0: c * ncols + lcol0 + M], rhs_list[c],
                 start=(first and c == 0), stop=(last and c == kc - 1))
        return slot, kc, ncols

    def conv4(l, acc, out_final, src_slot, wname, bname, c, nch, halo_v, src_r):
        P.copy(src_slot[:, 0:HALO].r() if src_r else src_slot[:, 0:HALO], halo_v)
        w = lambda k: CVt[l][:, CV[wname] + k * nch + c: CV[wname] + k * nch + c + 1]
        b = CVt[l][:, CV[bname] + c: CV[bname] + c + 1]
        P.ts(acc, src_slot[:, 3:3 + T], w(3), ALU.mult, b, ALU.add)
        for k in (2, 1):
            P.stt(acc, src_slot[:, k:k + T], w(k), acc, ALU.mult, ALU.add)
        P.stt(out_final, src_slot[:, 0:T], w(0), acc, ALU.mult, ALU.add)
        P.copy(halo_v.r() if src_r else halo_v, src_slot[:, T:T + 3])

    for ti in range(n_tiles):
        t0 = ti * T
        for c in range(8):
            P.dma(X[c], xT[c * 128:(c + 1) * 128, t0:t0 + T])
        for l in range(depth):
            P.dma(WSt[:, 0:1280], wsm_d[l][:, 0:1280], queue="pool")
            P.dma(WSt[:, 1280:2560], wsm_d[l][:, 1280:2560], queue="pool")
            RSt = TMP[2]
            dense_banks[0] = [0, 1, 4, 5, 6, 7]
            rmsnorm_stats(X, 8, D, RSt)
            for c in range(8):
                P.stt(H[c].r(), X[c], cvc(l, "g_mix_pre", c), RSt, ALU.mult, ALU.mult)
            dump("h", ti, l, H)
            if stop == "h":
                break
            YM = RS_[0:4]
            YR = RS_[4:8]
            YS = RS_[8:12]
            W_ = RS_[12:28]

            XM, XC, QT, KT = W_[0:4], W_[4:8], W_[8:12], W_[12:16]
            for i in range(4):
                ps = ps_dense()
                project(l, "xm%d" % i, H, ps)
                P.copy(XM[i][:, 3:3 + T].r(), ps, eng="act")
            if stop == "xm":
                break
            gblk = wget(l, "gates")
            IMr, Cr, Ar, EBr, DTr = ROW[0], ROW[1], ROW[2], ROW[3], ROW[4]
            ps = ps_dense()
            project(l, None, H, ps[0:4, :], M=4, lcol0=0, blk=gblk)
            P.copy(IMr[0:4, :], ps[0:4, :], eng="act")
            ps = ps_dense()
            project(l, None, H, ps[0:4, :], M=4, lcol0=4, blk=gblk)
            P.act(Cr[0:4, :], ps[0:4, :], AF.Exp, bias=NBF[l][0:4, 0:1], scale=-1.0)
            P.act(Cr[0:4, :], Cr[0:4, :], AF.Ln, bias=1.0)
            ps = ps_dense()
            project(l, None, H, ps[0:8, :], M=8, lcol0=8, blk=gblk)
            P.act(DTr[0:8, :], ps[0:8, :], AF.Exp, bias=cvc(l, "dt_bias", rows=8))
            P.act(DTr[0:8, :], DTr[0:8, :], AF.Ln, bias=1.0)
            P.scan(Cr[0:4, :], RESET[0:4, :], Cr[0:4, :], 0.0, ALU.mult, ALU.add)
            P.tt(Ar[0:4, :], IMr[0:4, :], Cr[0:4, :], ALU.add)
            P.ts(Ar[0:4, :], Ar[0:4, :], cvc(l, "b_i", rows=4), ALU.add, -0.5 * float(np.log(128.0)), ALU.add)
            P.act(Ar[0:4, :], Ar[0:4, :], AF.Exp)
            P.act(EBr[0:4, :], Cr[0:4, :], AF.Exp, scale=-1.0)
            dump("mrows", ti, l, [Cr[0:4, :], Ar[0:4, :], EBr[0:4, :], DTr[0:8, :]])
            for c in range(NCH):
                pst = ps_small()
                P.transpose(pst[0:64, 0:4], Ar[0:4, c * LC:(c + 1) * LC], IDENT[0:4, 0:4])
                P.copy(GT[:, c * 32: c * 32 + 4], pst[0:64, 0:4])
            if stop == "gates":
                break
            cacc = FS[2][:, 0:T]
            for i in range(4):
                conv4(l, cacc, cacc, XM[i], "cmw", "cmb", i, 4, HMR[l][:, i * 3:i * 3 + 3], True)
                P.act(XC[i][:, 0:T].r(), cacc, AF.Silu)
            dump("xc", ti, l, [v[:, 0:T] for v in XC])
            if stop == "conv":
                break
            for h in range(4):
                ps = ps_dense()
                P.mm(ps, WSt[:, WS_Q + h * 128: WS_Q + (h + 1) * 128], XC[h][:, 0:T])
                P.copy(QT[h][:, 0:T].r(), ps, eng="act")
                ps = ps_dense()
                P.mm(ps, WSt[:, WS_K + h * 128: WS_K + (h + 1) * 128], XC[h][:, 0:T])
                P.copy(KT[h][:, 0:T].r(), ps, eng="act")
            if stop == "qk":
                break
            EBT = [[TMP[0], TMP[1]], [FS[7][:, 0:T], FS[8][:, 0:T]]]
            post_q = []
            dense_banks[0] = [0, 1]
            for hp in range(2):
                heads = (2 * hp, 2 * hp + 1)
                NUM = {heads[0]: PL[0], heads[1]: PL[1]}
                DEN = {heads[0]: PL[2], heads[1]: PL[3]}
                EBS = {heads[0]: EBT[hp][0], heads[1]: EBT[hp][1]}
                for h in heads:
                    ps = ps_dense()
                    bcast_row(ps, EBr, h, 4, ROW[0])
                    P.copy(EBS[h], ps, eng="act")
                if stop == "ebs":
                    break
                for c in range(NCH):
                    cs = slice(c * LC, (c + 1) * LC)
                    if stop in ("c0", "c0a", "c0a1", "c0b", "c0c") and c == 1:
                        break
                    pkk = ps_dense()
                    pkv = ps_dense()
                    for q, h in enumerate(heads):
                        P.mm(pkk[0:64, q * 128:(q + 1) * 128], XC[h][:, cs], WSt[:, WS_K + h * 128: WS_K + (h + 1) * 128])
                    for q, h in enumerate(heads):
                        P.mm(pkv[0:64, q * 128:(q + 1) * 128], XM[h][:, 3 + c * LC: 3 + (c + 1) * LC],
                             WSt[:, WS_V + h * 128: WS_V + (h + 1) * 128])
                    psts = []
                    for q, h in enumerate(heads):
                        pst = ps_small()
                        P.mm(pst[0:64, 0:64], KT[h][:, cs], QT[h][:, cs])
                        psts.append(pst)
                    KA, VE = TK[(c % 2) * 2], TK[(c % 2) * 2 + 1]
                    P.tt(KA.rr("p (h d) -> p h d", h=2).r(), pkk[0:64, 0:256].rr("p (h d) -> p h d", h=2),
                         GT[:, c * 32 + heads[0]: c * 32 + heads[0] + 2].us(2).bc([64, 2, 128]), ALU.mult)
                    P.copy(VE.r(), pkv[0:64, 0:256], eng="act")
                    ats = []
                    for q, h in enumerate(heads):
                        at = AT_[(2 * c + q) % 4]
                        P.stt(at.r(), psts[q][0:64, 0:64], GT[:, c * 32 + h: c * 32 + h + 1], MASKT, ALU.mult, ALU.mult)
                        ats.append(at)
                    for q, h in enumerate(heads):
                        at = ats[q]
                        P.mm(NUM[h][:, cs], VE[:, q * 128:(q + 1) * 128], at, start=True, stop=False)
                        P.mm(NUM[h][:, cs], CS[l][h][:, 0:128], QT[h][:, cs], start=False, stop=True)
                        P.mm(DEN[h][:, cs], ONESR[0:64, :], at, start=True, stop=False)
                        P.mm(DEN[h][:, cs], CS[l][h][:, 128:256], QT[h][:, cs], start=False, stop=True)
                    for q, h in enumerate(heads):
                        pu = ps_small()
                        P.mm(pu[:, 0:128], KA[:, q * 128:(q + 1) * 128], VE[:, q * 128:(q + 1) * 128])
                        P.mm(pu[:, 128:256], KA[:, q * 128:(q + 1) * 128], ONESR[0:64, :])
                        P.tt(CSTMP[q], CS[l][h], pu[:, 0:256], ALU.add)
                        P.ts(CS[l][h].r(), CSTMP[q], EBS[h][:, c * LC + LC - 1: c * LC + LC], ALU.mult)
                    for _ in range(4):
                        if post_q:
                            post_q.pop(0)()
                while post_q:
                    post_q.pop(0)()
                if stop in ("c0", "cloop", "c0a", "c0a1", "c0b", "c0c"):
                    break
                for q, h in enumerate(heads):
                    if hp == 0:
                        nsrc, dsrc = FS[3 + 2 * q][:, 0:T], FS[4 + 2 * q][:, 0:T]
                        P.copy(nsrc, NUM[h], eng="act")
                        P.copy(dsrc, DEN[h], eng="act")
                    else:
                        nsrc, dsrc = NUM[h], DEN[h]

                    def mk(h=h, nsrc=nsrc, dsrc=dsrc, ebs=EBS[h]):
                        dn, t2, sg = TMP[2], FS[0][:, 0:T], FS[1][:, 0:T]
                        st = {}

                        def f_proj():
                            st["ps"] = ps_dense()
                            project(l, "om%d" % h, H, st["ps"])
                            P.act(sg, st["ps"], AF.Sigmoid)

                        def f_norm_mm():
                            P.act(SQ[0].r(), t2, AF.Square)
                            ps = ps_dense()
                            P.mm(ps, ONESR, SQ[0])
                            P.act(dn, ps, AF.Ln, bias=EPS, scale=1.0 / 128)
                        return [
                            f_proj,
                            lambda: P.tt(dn, dsrc, ebs, ALU.mult),
                            lambda: P.ts(t2, dn, -1.0, ALU.mult, 1.0, ALU.max),
                            lambda: P.tt(t2, t2, dn, ALU.max),
                            lambda: P.act(t2, t2, AF.Ln),
                            lambda: P.act(t2, t2, AF.Exp, scale=-1.0),
                            lambda: P.tt(t2, t2, ebs, ALU.mult),
                            lambda: P.tt(t2, nsrc, t2, ALU.mult),
                            lambda: P.tt(t2, t2, sg, ALU.mult),
                            f_norm_mm,
                            lambda: P.act(dn, dn, AF.Exp, scale=-0.5),
                            lambda: P.stt(YM[h][:, 0:T].r(), t2, cvc(l, "norm_m", h), dn, ALU.mult, ALU.mult),
                        ]
                    post_q.extend(mk())
            while post_q:
                post_q.pop(0)()
            dense_banks[0] = [0, 1, 4, 5, 6, 7]
            dump("ym", ti, l, [y[:, 0:T] for y in YM])
            if stop in ("mlstm", "ebs", "c0", "cloop", "c0a", "c0a1", "c0b", "c0c"):
                break

            XR = FS[0:4]
            RT = [[FS[4 + 4 * b + k][:, 0:T] for k in range(4)] for b in range(2)]
            cacc = TMP[0]
            XCR = W_[0:4]
            for i in range(4):
                ps = ps_dense()
                project(l, "xr%d" % i, H, ps)
                P.copy(XR[i][:, 3:3 + T], ps, eng="act")

            def rg_a(j):
                Rt, It, At, St = RT[j % 2]
                conv4(l, cacc, XCR[j][:, 0:T].r(), XR[j], "crw", "crb", j, 4, HM[l][:, j * 3:j * 3 + 3], False)
                ps = ps_dense()
                P.mm(ps, WSt[:, WS_A + j * 128: WS_A + (j + 1) * 128], XCR[j][:, 0:T])
                P.act(Rt, ps, AF.Sigmoid, bias=cvc(l, "b_a", j))
                ps = ps_dense()
                P.mm(ps, WSt[:, WS_X + j * 128: WS_X + (j + 1) * 128], XCR[j][:, 0:T])
                P.act(It, ps, AF.Sigmoid, bias=cvc(l, "b_x", j))
                P.act(At, Rt, AF.Exp, scale=NCF[l][:, j:j + 1])
                P.act(St, At, AF.Square)
                P.act(St, St, AF.Sqrt, bias=1.0, scale=-1.0)

            def rg_b(j):
                Rt, It, At, St = RT[j % 2]
                P.tt(It, It, XCR[j][:, 0:T], ALU.mult)
                P.tt(It, It, St, ALU.mult)
                P.scan(Rt, At, It, HST[l][:, j:j + 1], ALU.mult, ALU.add)
                P.copy(HST[l][:, j:j + 1], Rt[:, T - 1:T])
                ps = ps_dense()
                project(l, "yr%d" % j, H, ps)
                P.act(St, ps, AF.Gelu_apprx_tanh)
                P.tt(YR[j][:, 0:T].r(), Rt, St, ALU.mult)
            rg_a(0)
            for j in range(4):
                if j + 1 < 4:
                    rg_a(j + 1)
                rg_b(j)
            dump("yr", ti, l, [y[:, 0:T] for y in YR])
            if stop == "rglru":
                break

            XBC = FS[0:8]
            cacc = FS[8][:, 0:T]
            XSC = W_[0:8]
            for i in range(8):
                ps = ps_dense()
                project(l, "xbc%d" % i, H, ps)
                P.copy(XBC[i][:, 3:3 + T], ps, eng="act")
            for i in range(8):
                conv4(l, cacc, cacc, XBC[i], "csw", "csb", i, 8, HM[l][:, 12 + i * 3: 12 + i * 3 + 3], False)
                P.act(XSC[i][:, 0:T].r(), cacc, AF.Silu)
            XS, BT, CT = [v[:, 0:T] for v in XSC[0:4]], [v[:, 0:T] for v in XSC[4:6]], [v[:, 0:T] for v in XSC[6:8]]
            ACN, Wr = ROW[0], ROW[1]
            P.ts(ACN[0:8, :], DTr[0:8, :], NBF[l][0:8, 1:2], ALU.mult)
            P.scan(ACN[0:8, :], RESET[0:8, :], ACN[0:8, :], 0.0, ALU.mult, ALU.add)
            P.tt(Wr[0:8, :].rr("p (c t) -> p c t", t=LC),
                 ACN[0:8, :].rr("p (c t) -> p c t", t=LC)[:, :, LC - 1:LC].bc([8, NCH, LC]),
                 ACN[0:8, :].rr("p (c t) -> p c t", t=LC), ALU.subtract)
            P.act(Wr[0:8, :], Wr[0:8, :], AF.Exp, scale=-1.0)
            P.tt(Wr[0:8, :], Wr[0:8, :], DTr[0:8, :], ALU.mult)
            dump("srows", ti, l, [ACN[0:8, :], Wr[0:8, :], DTr[0:8, :]])
            for c in range(NCH):
                pst = ps_small()
                cs = slice(c * LC, (c + 1) * LC)
                P.transpose(pst[0:64, 0:8], ACN[0:8, cs], IDENT[0:8, 0:8])
                P.transpose(pst[0:64, 8:16], DTr[0:8, cs], IDENT[0:8, 0:8])
                P.transpose(pst[0:64, 16:24], Wr[0:8, cs], IDENT[0:8, 0:8])
                P.copy(GT[:, c * 32 + 4: c * 32 + 28], pst[0:64, 0:24])
            GT3 = GT.rr("p (c k) -> p c k", k=32)
            GY = [FS[i][:, 0:T] for i in range(4)]
            dense_banks[0] = [0, 1]
            for g in range(2):
                pcb = ps_dense()
                for c in range(NCH):
                    cs = slice(c * LC, (c + 1) * LC)
                    P.mm(pcb[0:64, cs], BT[g][:, cs], CT[g][:, cs])
                P.tt(CBM.rr("p (c t) -> p c t", t=LC), pcb[0:64, :].rr("p (c t) -> p c t", t=LC),
                     MASKT.us(1).bc([64, NCH, LC]), ALU.mult)
                EAC2 = [TMP[0], TMP[1]]
                EATOT = SMALL
                for q in range(4):
                    h = 4 * g + q
                    pa = ps_dense()
                    bcast_row(pa, ACN, h, 8, ROW[2])
                    half = slice(64 * (q % 2), 64 * (q % 2) + 64)
                    P.act(EAC2[q // 2][half, :], pa[half, :], AF.Exp, scale=-1.0)
                    P.act(EATOT[:, q * 8:(q + 1) * 8], pa.rr("p (c t) -> p c t", t=LC)[:, :, LC - 1], AF.Exp, scale=-1.0)
                    P.tt(SEGT.rr("p (c t) -> p c t", t=LC), GT3[:, :, 4 + h:5 + h].bc([64, NCH, LC]),
                         pa[0:64, :].rr("p (c t) -> p c t", t=LC), ALU.subtract)
                    P.ts(SEGT, SEGT, 0.0, ALU.min)
                    P.act(SEGT, SEGT, AF.Exp)
                    P.tt(SEGT.rr("p (c t) -> p c t", t=LC), SEGT.rr("p (c t) -> p c t", t=LC),
                         GT3[:, :, 12 + h:13 + h].bc([64, NCH, LC]), ALU.mult)
                    P.tt(MT[q].r(), SEGT, CBM, ALU.mult)
                YD = [PL[0], PL[1]]
                YO = [PL[2], PL[3]]
                for c in range(NCH):
                    cs = slice(c * LC, (c + 1) * LC)
                    pxt = ps_small()
                    for jj in range(2):
                        P.transpose(pxt[0:64, jj * 128:(jj + 1) * 128], XS[2 * g + jj][:, cs], IDENT)
                    P.transpose(pxt[0:64, 256:384], BT[g][:, cs], IDENT)
                    XT_, XW, BTK = TK[(c % 2) * 3], TK[(c % 2) * 3 + 1], TK[(c % 2) * 3 + 2]
                    P.copy(XT_.r(), pxt[0:64, 0:256], eng="act")
                    P.tt(XW.rr("p (h d) -> p h d", h=4).r(), pxt[0:64, 0:256].rr("p (h d) -> p h d", h=4),
                         GT[:, c * 32 + 20 + 4 * g: c * 32 + 24 + 4 * g].us(2).bc([64, 4, 64]), ALU.mult)
                    P.copy(BTK[:, 0:128].r(), pxt[0:64, 256:384], eng="act")
                    for jj in range(2):
                        j = 2 * g + jj
                        for q2 in range(2):
                            q = 2 * jj + q2
                            P.mm(YD[jj][64 * q2:64 * q2 + 64, cs], XT_[:, q * 64:(q + 1) * 64], MT[q][:, cs], r32=False)
                        P.mm(YO[jj][:, cs], SST[l][j], CT[g][:, cs])
                    pu = ps_small()
                    P.mm(pu[:, 0:256], BTK[:, 0:128], XW)
                    for jj in range(2):
                        j = 2 * g + jj
                        for q2 in range(2):
                            q = 2 * jj + q2
                            P.stt(SST[l][j][:, q2 * 64:(q2 + 1) * 64].r(), SST[l][j][:, q2 * 64:(q2 + 1) * 64],
                                  EATOT[:, q * 8 + c: q * 8 + c + 1], pu[:, q * 64:(q + 1) * 64], ALU.mult, ALU.add)
                for jj in range(2):
                    j = 2 * g + jj
                    y, sz = TMP[2], FS[9][:, 0:T]
                    ps = ps_dense()
                    project(l, "zs%d" % j, H, ps)
                    P.act(sz, ps, AF.Silu)
                    P.tt(y, YO[jj], EAC2[jj], ALU.mult)
                    P.tt(y, y, YD[jj], ALU.add)
                    P.stt(y, XS[j], cvc(l, "dskip", j), y, ALU.mult, ALU.add)
                    P.tt(GY[j], y, sz, ALU.mult)
            dense_banks[0] = [0, 1, 4, 5, 6, 7]
            rs = TMP[2]
            rmsnorm_stats(GY, 4, 512, rs)
            for j in range(4):
                P.stt(YS[j][:, 0:T].r(), GY[j], cvc(l, "norm_s", j), rs, ALU.mult, ALU.mult)
            dump("ys", ti, l, [y[:, 0:T] for y in YS])
            if stop == "ssd":
                break

            YALL = [y[:, 0:T] for y in (YM + YR + YS)]
            MIX = [s[:, 0:T] for s in FS[0:8]]
            for n in range(8):
                ps = ps_dense()
                project(l, "out%da" % n, YALL[0:8], ps, last=False)
                project(l, "out%db" % n, YALL[8:12], ps, first=False)
                P.copy(MIX[n], ps, eng="act")
            rs = TMP[2]
            rmsnorm_stats(MIX, 8, D, rs)
            for c in range(8):
                P.stt(MIX[c], MIX[c], cvc(l, "g_mix_post", c), rs, ALU.mult, ALU.mult)
                P.tt(X[c], X[c], MIX[c], ALU.add)
            dump("x1", ti, l, X)
            if stop == "mix":
                break

            rs = TMP[2]
            rmsnorm_stats(X, 8, D, rs)
            for c in range(8):
                P.stt(H[c].r(), X[c], cvc(l, "g_ffn_pre", c), rs, ALU.mult, ALU.mult)
            ACTF = [s[:, 0:T] for s in RS_[0:NJ]]
            UB = FS[0:4]
            DST = [[TMP[0], TMP[1]], [FS[4][:, 0:T], FS[5][:, 0:T]]]
            dense_banks[0] = [0, 1, 4, 5, 6, 7]
            for j in range(NJ + 1):
                if j < NJ:
                    for q, nm in enumerate(("upg", "upv")):
                        ps = ps_dense()
                        project(l, "%s%d" % (nm, j), H, ps)
                        ub = UB[(j % 2) * 2 + q]
                        cidx = q * NJ + j
                        w = lambda k: CVt[l][:, CV["cfw"] + k * 44 + cidx: CV["cfw"] + k * 44 + cidx + 1]
                        b = CVt[l][:, CV["cfb"] + cidx: CV["cfb"] + cidx + 1]
                        dst = DST[j % 2][q]
                        P.act(dst, ps, AF.Identity, bias=b, scale=w(2))
                        P.copy(ub[:, 3:3 + T], ps, eng="act")
                        P.copy(ub[:, 1:3], HF[l][:, cidx * 2: cidx * 2 + 2], eng=HALO_ENG)
                        P.stt(dst, ub[:, 2:2 + T], w(1), dst, ALU.mult, ALU.add)
                        P.stt(dst, ub[:, 1:1 + T], w(0), dst, ALU.mult, ALU.add)
                        P.copy(HF[l][:, cidx * 2: cidx * 2 + 2], ub[:, T + 1:T + 3], eng=HALO_ENG)
                if j > 0:
                    rg, rv = DST[(j - 1) % 2]
                    P.act(rg, rg, AF.Gelu_apprx_tanh)
                    P.tt(ACTF[j - 1].r(), rg, rv, ALU.mult)
            FO = [s[:, 0:T] for s in FS[4:12]]
            for n in range(8):
                ps = ps_dense()
                project(l, "dn%da" % n, ACTF[0:8], ps, last=False)
                project(l, "dn%db" % n, ACTF[8:16], ps, first=False, last=False)
                project(l, "dn%dc" % n, ACTF[16:22], ps, first=False)
                P.copy(FO[n], ps, eng="act")
            rs = TMP[2]
            rmsnorm_stats(FO, 8, D, rs)
            for c in range(8):
                P.stt(FO[c], FO[c], cvc(l, "g_ffn_post", c), rs, ALU.mult, ALU.mult)
                P.tt(X[c], X[c], FO[c], ALU.add)
            dump("x2", ti, l, X)
        else:
            for c in range(8):
                final_evs.append(P.dma(outT[c * 128:(c + 1) * 128, t0:t0 + T], X[c]))
            continue
        break
    P.emit(final_events=[e for e in final_evs if e is not None])
    return nc, P


_CACHE = {}


def kernel(**inputs):
    x = np.asarray(inputs["x"], np.float32)
    B = x.shape[0]
    wbig, wsm, cvec = host_layout(inputs)
    cmat = host_consts()
    if "nc" not in _CACHE:
        _CACHE["nc"] = build()[0]
    nc = _CACHE["nc"]
    in_maps = []
    for b in range(B):
        m = {"xT": np.ascontiguousarray(x[b].T), "cvec": cvec, "cmat": cmat}
        for l in range(DEPTH):
            m["wbig%d" % l] = wbig[l]
            m["wsm%d" % l] = wsm[l]
        in_maps.append(m)
    res = run_bass_kernel_spmd(nc, in_maps, core_ids=list(range(B)))
    out = np.stack([np.ascontiguousarray(r["outT"].T) for r in res.results], axis=0)
    return out.astype(np.float32)
```

```python
import contextlib
import numpy as np
import concourse.bass as bass
import concourse.mybir as mybir
from concourse.bass_utils import run_bass_kernel_spmd

F32 = mybir.dt.float32
F32R = mybir.dt.float32r
AF = mybir.ActivationFunctionType
ALU = mybir.AluOpType

SAME_ENGINE_SYNC = True
HALO_ENG = "dve"


class Buf:
    __slots__ = ("name", "lastw", "readers", "dsem", "dcount", "psum")

    def __init__(self, name, psum=False):
        self.name = name
        self.psum = psum
        self.lastw = None
        self.readers = []
        self.dsem = None
        self.dcount = 0


class V:
    __slots__ = ("buf", "ap")

    def __init__(self, buf, ap):
        self.buf = buf
        self.ap = ap

    def __getitem__(self, k):
        return V(self.buf, self.ap[k])

    def rr(self, s, **kw):
        return V(self.buf, self.ap.rearrange(s, **kw))

    def bc(self, shape):
        return V(self.buf, self.ap.to_broadcast(list(shape)))

    def us(self, axis):
        return V(self.buf, self.ap.unsqueeze(axis))

    def r(self):
        return V(self.buf, self.ap.bitcast(F32R))

    def f(self):
        return V(self.buf, self.ap.bitcast(F32))

    @property
    def shape(self):
        return self.ap.shape


def _ap(x):
    return x.ap if isinstance(x, V) else x


class Prog:
    ENGS = ("pe", "act", "dve", "pool", "sp")

    def __init__(self, nc):
        self.nc = nc
        self.stack = contextlib.ExitStack()
        self.ops = {e: [] for e in self.ENGS}
        self.sems = {}
        self.ecount = {e: 0 for e in self.ENGS}
        self.seen = {e: {} for e in self.ENGS}
        for e in ("pe", "act", "dve", "pool"):
            self.sems[e] = self.stack.enter_context(nc.semaphore("e_" + e))
        self.ctotal = {}
        for q in ("sp", "pool"):
            self.sems["const_" + q] = self.stack.enter_context(nc.semaphore("consts_" + q))
            self.ctotal[q] = 0
        self.n_inst = 0

    def sbuf(self, name, shape, dtype=F32):
        t = self.stack.enter_context(self.nc.sbuf_tensor(name, list(shape), dtype))
        return V(Buf(name), t[:])

    def psum(self, name, shape, dtype=F32):
        t = self.stack.enter_context(self.nc.psum_tensor(name, list(shape), dtype))
        return V(Buf(name, psum=True), t[:])

    def _collect(self, eng, reads, writes):
        need = {}

        def add(ev):
            if ev is None:
                return
            k, val = ev
            if need.get(k, 0) < val:
                need[k] = val
        for b in reads:
            add(b.lastw)
            if b.psum:
                for ev in b.readers:
                    if ev[0] != eng:
                        add(ev)
        for b in writes:
            add(b.lastw)
            for ev in b.readers:
                add(ev)
        waits = []
        for k, val in need.items():
            if k == eng and (eng == "pe" or not SAME_ENGINE_SYNC):
                continue
            if self.seen[eng].get(k, 0) >= val:
                continue
            self.seen[eng][k] = val
            waits.append((k, val))
        return waits

    def _commit(self, ev, reads, writes):
        for b in writes:
            b.lastw = ev
            b.readers = []
        for b in reads:
            if b in writes:
                continue
            b.readers = [r for r in b.readers if r[0] != ev[0]] + [ev]

    def op(self, eng, fn, reads, writes):
        reads = list({id(v.buf): v.buf for v in reads if isinstance(v, V)}.values())
        writes = list({id(v.buf): v.buf for v in writes if isinstance(v, V)}.values())
        waits = self._collect(eng, reads, writes)
        self.ecount[eng] += 1
        ev = (eng, self.ecount[eng])
        self.ops[eng].append((waits, fn, (eng, 1)))
        self._commit(ev, reads, writes)
        self.n_inst += 1

    def dma(self, out, in_, queue="sp", const=False):
        reads = [in_.buf] if isinstance(in_, V) else []
        writes = [out.buf] if isinstance(out, V) else []
        waits = self._collect(queue, reads, writes)
        o, i = _ap(out), _ap(in_)
        fn = lambda e, o=o, i=i: e.dma_start(out=o, in_=i)
        if const:
            self.ctotal[queue] += 16
            self.ops[queue].append((waits, fn, ("const_" + queue, 16), True))
            return None
        b = (writes + reads)[0]
        if b.dsem is None:
            key = "d%d" % len(self.sems)
            self.sems[key] = self.stack.enter_context(self.nc.semaphore(key))
            b.dsem = key
        b.dcount += 16
        ev = (b.dsem, b.dcount)
        self.ops[queue].append((waits, fn, (b.dsem, 16)))
        self._commit(ev, reads, writes)
        self.n_inst += 1
        return ev

    def mm(self, out, lhsT, rhs, start=True, stop=True, r32=True):
        l = lhsT.ap.bitcast(F32R) if r32 else lhsT.ap
        r = rhs.ap.bitcast(F32R) if r32 else rhs.ap
        o = out.ap
        self.op("pe", lambda e: e.matmul(o, l, r, start=start, stop=stop), [lhsT, rhs], [out])

    def transpose(self, out, in_, ident):
        o, i, d = out.ap, in_.ap, ident.ap
        self.op("pe", lambda e: e.transpose(o, i, d), [in_, ident], [out])

    def act(self, out, in_, func, bias=0.0, scale=1.0):
        o, i = out.ap, in_.ap
        b, s = _ap(bias), _ap(scale)
        rd = [in_] + [x for x in (bias, scale) if isinstance(x, V)]
        self.op("act", lambda e: e.activation(out=o, in_=i, func=func, bias=b, scale=s), rd, [out])

    def tt(self, out, in0, in1, op, eng="dve"):
        o, a, b = out.ap, in0.ap, in1.ap
        self.op(eng, lambda e: e.tensor_tensor(out=o, in0=a, in1=b, op=op), [in0, in1], [out])

    def ts(self, out, in0, s1, op0, s2=None, op1=None, eng="dve"):
        o, a = out.ap, in0.ap
        x1, x2 = _ap(s1), _ap(s2)
        rd = [in0] + [x for x in (s1, s2) if isinstance(x, V)]
        if op1 is None:
            self.op(eng, lambda e: e.tensor_scalar(out=o, in0=a, scalar1=x1, scalar2=None, op0=op0), rd, [out])
        else:
            self.op(eng, lambda e: e.tensor_scalar(out=o, in0=a, scalar1=x1, scalar2=x2, op0=op0, op1=op1),
                    rd, [out])

    def stt(self, out, in0, scalar, in1, op0, op1):
        o, a, b = out.ap, in0.ap, in1.ap
        s = _ap(scalar)
        rd = [in0, in1] + ([scalar] if isinstance(scalar, V) else [])
        self.op("dve", lambda e: e.scalar_tensor_tensor(out=o, in0=a, scalar=s, in1=b, op0=op0, op1=op1),
                rd, [out])

    def copy(self, out, in_, eng="dve"):
        o, i = out.ap, in_.ap
        if eng == "act":
            self.op("act", lambda e: e.activation(out=o, in_=i, func=AF.Copy), [in_], [out])
        else:
            self.op(eng, lambda e: e.tensor_copy(out=o, in_=i), [in_], [out])

    def recip(self, out, in_):
        o, i = out.ap, in_.ap
        self.op("dve", lambda e: e.reciprocal(out=o, in_=i), [in_], [out])

    def scan(self, out, d0, d1, initial, op0, op1):
        o, a, b = out.ap, d0.ap, d1.ap
        ini = _ap(initial)
        rd = [d0, d1] + ([initial] if isinstance(initial, V) else [])
        self.op("dve", lambda e: e.tensor_tensor_scan(out=o, data0=a, data1=b, initial=ini, op0=op0, op1=op1),
                rd, [out])

    def memset(self, out, val, eng="dve"):
        o = out.ap
        self.op(eng, lambda e: e.memset(o, val), [], [out])

    def emit(self, final_events=()):
        nc = self.nc
        sems = self.sems
        ops = self.ops
        ctotal = self.ctotal

        def run(e, name):
            first = True
            for rec in ops[name]:
                waits, fn, inc = rec[:3]
                if first and len(rec) == 3:
                    for q, tot in ctotal.items():
                        if tot > 0:
                            e.wait_ge(sems["const_" + q], tot)
                    first = False
                for k, val in waits:
                    e.wait_ge(sems[k], val)
                fn(e).then_inc(sems[inc[0]], inc[1])
            if name == "sp":
                for k, val in final_events:
                    e.wait_ge(sems[k], val)

        with nc.Block() as block:
            @block.sync
            def _(e):
                run(e, "sp")

            @block.tensor
            def _(e):
                run(e, "pe")

            @block.scalar
            def _(e):
                run(e, "act")

            @block.vector
            def _(e):
                run(e, "dve")

            @block.gpsimd
            def _(e):
                run(e, "pool")
        self.stack.close()


D = 1024
S = 4096
DEPTH = 2
T = 512
NT = S // T
LC = 64
NCH = T // LC
DFF = 2816
NJ = DFF // 128
EPS = 1e-6
HALO = 3
SW = T + HALO

C_XM, C_OM, C_IM, C_FM, C_XR, C_YR, C_ZS, C_XBC, C_DT = 0, 512, 1024, 1028, 1032, 1544, 2056, 2568, 3592

WB = {}
_off = 0


def _wb(name, kc, ncols=128):
    global _off
    WB[name] = (_off, kc, ncols)
    _off += kc * ncols


for _i in range(4):
    _wb("xm%d" % _i, 8)
for _i in range(4):
    _wb("om%d" % _i, 8)
_wb("gates", 8, 16)
for _i in range(4):
    _wb("xr%d" % _i, 8)
for _i in range(4):
    _wb("yr%d" % _i, 8)
for _i in range(4):
    _wb("zs%d" % _i, 8)
for _i in range(8):
    _wb("xbc%d" % _i, 8)
for _n in range(8):
    _wb("out%da" % _n, 8)
    _wb("out%db" % _n, 4)
for _j in range(NJ):
    _wb("upg%d" % _j, 8)
    _wb("upv%d" % _j, 8)
for _n in range(8):
    _wb("dn%da" % _n, 8)
    _wb("dn%db" % _n, 8)
    _wb("dn%dc" % _n, 6)
WCOLS = _off
WSEQ = (["xm%d" % i for i in range(4)] + ["gates"] + ["om%d" % i for i in range(4)]
        + ["xr%d" % i for i in range(4)] + ["yr%d" % i for i in range(4)]
        + ["xbc%d" % i for i in range(8)] + ["zs%d" % i for i in range(4)]
        + [n for n in WB if n.startswith("out")] + [n for n in WB if n.startswith("up")]
        + [n for n in WB if n.startswith("dn")])
assert len(WSEQ) == len(WB)

WS_Q, WS_K, WS_V, WS_A, WS_X = 0, 512, 1024, 1536, 2048
WSCOLS = 2560

CV = {}
_cvo = 0


def _cv(name, n):
    global _cvo
    CV[name] = _cvo
    _cvo += n


for _n, _k in [("g_mix_pre", 8), ("g_mix_post", 8), ("g_ffn_pre", 8), ("g_ffn_post", 8),
               ("cmw", 16), ("cmb", 4), ("norm_m", 4),
               ("crw", 16), ("crb", 4), ("b_a", 4), ("b_x", 4), ("lam", 4),
               ("csw", 32), ("csb", 8), ("norm_s", 4), ("dskip", 4),
               ("cfw", 3 * 44), ("cfb", 44),
               ("b_i", 1), ("b_f", 1), ("dt_bias", 1), ("a_log", 1)]:
    _cv(_n, _k)
NCV = _cvo

CM_ID, CM_ONES, CM_MASK, CM_RESET, CM_SEL = 0, 128, 256, 320, 832
NCM = 832


def host_consts():
    cm = np.zeros((128, NCM), np.float32)
    cm[:, CM_ID:CM_ID + 128] = np.eye(128, dtype=np.float32)
    cm[:, CM_ONES:CM_ONES + 128] = 1.0
    s = np.arange(64)[:, None]
    t = np.arange(64)[None, :]
    cm[:64, CM_MASK:CM_MASK + 64] = (s <= t).astype(np.float32)
    r = np.ones(512, np.float32)
    r[::64] = 0.0
    cm[:, CM_RESET:CM_RESET + 512] = r[None, :]
    return cm


def _pc(v, n):
    return np.ascontiguousarray(np.asarray(v, np.float32).reshape(n, 128).T)


def _blk(W, col0, kc, ncols=128, row0=0):
    sub = W[row0:row0 + kc * 128, col0:col0 + ncols]
    return np.ascontiguousarray(sub.reshape(kc, 128, ncols).transpose(1, 0, 2).reshape(128, kc * ncols))


def host_layout(inp):
    wbig, wsm = [], []
    cvec = np.zeros((DEPTH, 128, NCV), np.float32)
    for l in range(DEPTH):
        w_in, w_out, w_up, w_dn = (np.asarray(inp[k][l], np.float32) for k in ("w_in", "w_out", "w_up", "w_down"))
        big = np.zeros((128, WCOLS), np.float32)

        def put(name, arr):
            o, kc, ncols = WB[name]
            big[:, o:o + kc * ncols] = arr
        for i in range(4):
            put("xm%d" % i, _blk(w_in, C_XM + i * 128, 8))
            put("om%d" % i, _blk(w_in, C_OM + i * 128, 8))
            put("xr%d" % i, _blk(w_in, C_XR + i * 128, 8))
            put("yr%d" % i, _blk(w_in, C_YR + i * 128, 8))
            put("zs%d" % i, _blk(w_in, C_ZS + i * 128, 8))
        for i in range(8):
            put("xbc%d" % i, _blk(w_in, C_XBC + i * 128, 8))
        gcols = np.concatenate([w_in[:, C_IM:C_IM + 4], w_in[:, C_FM:C_FM + 4], w_in[:, C_DT:C_DT + 8]], axis=1)
        put("gates", _blk(gcols, 0, 8, 16))
        for n in range(8):
            put("out%da" % n, _blk(w_out, n * 128, 8, row0=0))
            put("out%db" % n, _blk(w_out, n * 128, 4, row0=1024))
            put("dn%da" % n, _blk(w_dn, n * 128, 8, row0=0))
            put("dn%db" % n, _blk(w_dn, n * 128, 8, row0=1024))
            put("dn%dc" % n, _blk(w_dn, n * 128, 6, row0=2048))
        for j in range(NJ):
            put("upg%d" % j, _blk(w_up, j * 128, 8))
            put("upv%d" % j, _blk(w_up, DFF + j * 128, 8))
        wbig.append(big)
        sm = np.zeros((128, WSCOLS), np.float32)
        for h in range(4):
            sm[:, WS_Q + h * 128: WS_Q + (h + 1) * 128] = inp["w_q_m"][l][h]
            sm[:, WS_K + h * 128: WS_K + (h + 1) * 128] = inp["w_k_m"][l][h]
            sm[:, WS_V + h * 128: WS_V + (h + 1) * 128] = inp["w_v_m"][l][h]
        for j in range(4):
            for q in range(2):
                sm[q * 64:(q + 1) * 64, WS_A + j * 128 + q * 64: WS_A + j * 128 + (q + 1) * 64] = inp["w_a_r"][l][2 * j + q]
                sm[q * 64:(q + 1) * 64, WS_X + j * 128 + q * 64: WS_X + j * 128 + (q + 1) * 64] = inp["w_x_r"][l][2 * j + q]
        wsm.append(sm)
        cv = cvec[l]
        for nm, key in [("g_mix_pre", "norm_mix_pre"), ("g_mix_post", "norm_mix_post"),
                        ("g_ffn_pre", "norm_ffn_pre"), ("g_ffn_post", "norm_ffn_post")]:
            cv[:, CV[nm]:CV[nm] + 8] = _pc(inp[key][l], 8)
        for k in range(4):
            cv[:, CV["cmw"] + k * 4: CV["cmw"] + k * 4 + 4] = _pc(inp["conv_m_w"][l][k], 4)
            cv[:, CV["crw"] + k * 4: CV["crw"] + k * 4 + 4] = _pc(inp["conv_r_w"][l][k], 4)
            cv[:, CV["csw"] + k * 8: CV["csw"] + k * 8 + 8] = _pc(inp["conv_s_w"][l][k], 8)
        for k in range(3):
            cv[:, CV["cfw"] + k * 44: CV["cfw"] + k * 44 + 44] = _pc(inp["conv_f_w"][l][k], 44)
        cv[:, CV["cfb"]:CV["cfb"] + 44] = _pc(inp["conv_f_b"][l], 44)
        cv[:, CV["cmb"]:CV["cmb"] + 4] = _pc(inp["conv_m_b"][l], 4)
        cv[:, CV["norm_m"]:CV["norm_m"] + 4] = _pc(inp["norm_m"][l], 4)
        cv[:, CV["crb"]:CV["crb"] + 4] = _pc(inp["conv_r_b"][l], 4)
        cv[:, CV["b_a"]:CV["b_a"] + 4] = _pc(inp["b_a_r"][l], 4)
        cv[:, CV["b_x"]:CV["b_x"] + 4] = _pc(inp["b_x_r"][l], 4)
        cv[:, CV["lam"]:CV["lam"] + 4] = _pc(inp["lam_r"][l], 4)
        cv[:, CV["csb"]:CV["csb"] + 8] = _pc(inp["conv_s_b"][l], 8)
        cv[:, CV["norm_s"]:CV["norm_s"] + 4] = _pc(inp["norm_s"][l], 4)
        cv[:, CV["dskip"]:CV["dskip"] + 4] = _pc(np.repeat(np.asarray(inp["d_skip_s"][l], np.float32), 64), 4)
        cv[0:4, CV["b_i"]] = inp["b_i_m"][l]
        cv[0:4, CV["b_f"]] = inp["b_f_m"][l]
        cv[0:8, CV["dt_bias"]] = inp["dt_bias_s"][l]
        cv[0:8, CV["a_log"]] = inp["a_log_s"][l]
    return wbig, wsm, cvec


def build(n_tiles=NT, depth=DEPTH, dbg=None, stop=None):
    nc = bass.Bass("TRN2", target_bir_lowering=False)
    xT = nc.dram_tensor("xT", [D, S], F32, kind="ExternalInput").ap()
    outT = nc.dram_tensor("outT", [D, S], F32, kind="ExternalOutput").ap()
    wbig_d = [nc.dram_tensor("wbig%d" % l, [128, WCOLS], F32, kind="ExternalInput").ap() for l in range(DEPTH)]
    wsm_d = [nc.dram_tensor("wsm%d" % l, [128, WSCOLS], F32, kind="ExternalInput").ap() for l in range(DEPTH)]
    cvec_d = nc.dram_tensor("cvec", [DEPTH, 128, NCV], F32, kind="ExternalInput").ap()
    cmat_d = nc.dram_tensor("cmat", [128, NCM], F32, kind="ExternalInput").ap()
    dbg = dbg or {}
    dbg_d = {}
    for name, (dt_, dl_, ncols) in dbg.items():
        dbg_d[name] = nc.dram_tensor("dbg_" + name, [128, ncols], F32, kind="ExternalOutput").ap()

    P = Prog(nc)
    final_evs = []

    CM = P.sbuf("CM", [128, NCM])
    P.dma(CM, cmat_d, const=True)
    ONESR = P.sbuf("ONESR", [128, 128], F32R)
    P.dma(ONESR, cmat_d[:, CM_ONES:CM_ONES + 128], queue="pool", const=True)
    IDENT = CM[:, CM_ID:CM_ID + 128]
    MASKT = CM[0:64, CM_MASK:CM_MASK + 64]
    RESET = CM[:, CM_RESET:CM_RESET + 512]

    def bcast_row(ps, rows, h, k, tmp):
        P.act(tmp[0:k, :], rows[0:k, :], AF.Copy, scale=IDENT[0:k, h:h + 1])
        P.mm(ps, CM[0:k, CM_ONES:CM_ONES + 128], tmp[0:k, :], r32=False)
    CVt = []
    for l in range(depth):
        cv = P.sbuf("CV%d" % l, [128, NCV])
        P.dma(cv, cvec_d[l], const=True)
        CVt.append(cv)
    WSt = P.sbuf("WS", [128, WSCOLS], F32R)

    def cvc(l, name, c=0, rows=128):
        o = CV[name] + c
        return CVt[l][0:rows, o:o + 1]

    X = [P.sbuf("X%d" % c, [128, T]) for c in range(8)]
    H = [P.sbuf("H%d" % c, [128, T]) for c in range(8)]
    RS_ = [P.sbuf("RS%d" % i, [128, SW]) for i in range(28)]
    FS = [P.sbuf("FS%d" % i, [128, SW]) for i in range(12)]
    NRING = 5
    RING = [P.sbuf("WR%d" % i, [128, 1024], F32R) for i in range(NRING)]
    TMP = [P.sbuf("TMP%d" % i, [128, T]) for i in range(3)]
    SQ = [P.sbuf("SQ%d" % i, [128, T]) for i in range(2)]
    ROW = [P.sbuf("ROW%d" % i, [8, T]) for i in range(5)]
    GT = P.sbuf("GT", [64, NCH * 32])
    TK = [P.sbuf("TK%d" % i, [64, 256]) for i in range(6)]
    AT_ = [P.sbuf("AT%d" % i, [64, 64]) for i in range(4)]
    MT = [P.sbuf("MT%d" % i, [64, T]) for i in range(4)]
    SEGT = P.sbuf("SEGT", [64, T])
    CBM = P.sbuf("CBM", [64, T])
    CSTMP = [P.sbuf("CSTMP%d" % i, [128, 256]) for i in range(2)]
    HF = [P.sbuf("HF%d" % l, [128, 44 * 2]) for l in range(depth)]
    HMR = [P.sbuf("HMR%d" % l, [128, 4 * 3]) for l in range(depth)]
    HM = [P.sbuf("HM%d" % l, [128, 12 * 3]) for l in range(depth)]
    CS = [[P.sbuf("CS%d_%d" % (l, h), [128, 256]) for h in range(4)] for l in range(depth)]
    SST = [[P.sbuf("SST%d_%d" % (l, j), [128, 128]) for j in range(4)] for l in range(depth)]
    HST = [P.sbuf("HST%d" % l, [128, 4]) for l in range(depth)]
    NCF = [P.sbuf("NCF%d" % l, [128, 4]) for l in range(depth)]
    NBF = [P.sbuf("NBF%d" % l, [8, 2]) for l in range(depth)]
    SMALL = P.sbuf("SMALL", [128, 64])

    PS = [P.psum("PS%d" % i, [128, T]) for i in range(8)]
    dense_rr = [0]
    small_rr = [0]

    dense_banks = [[0, 1]]

    def ps_dense():
        dense_rr[0] += 1
        lst = dense_banks[0]
        return PS[lst[dense_rr[0] % len(lst)]]

    def ps_small():
        small_rr[0] ^= 1
        return PS[2 + small_rr[0]]
    PL = PS[4:8]

    for l in range(depth):
        P.memset(HF[l], 0.0)
        P.memset(HM[l], 0.0)
        P.ts(HMR[l].r(), CM[:, 0:12], 0.0, ALU.mult)
        P.memset(HST[l], 0.0)
        for h in range(4):
            P.ts(CS[l][h].r(), CM[:, 0:256], 0.0, ALU.mult)
            P.ts(SST[l][h].r(), CM[:, 0:128], 0.0, ALU.mult)
        P.act(SMALL[:, 0:4], CVt[l][:, CV["lam"]:CV["lam"] + 4], AF.Exp, scale=-1.0)
        P.act(SMALL[:, 4:8], SMALL[:, 0:4], AF.Ln, bias=1.0)
        P.ts(NCF[l], SMALL[:, 4:8], -8.0, ALU.mult)
        P.ts(NBF[l][0:4, 0:1], cvc(l, "b_f", rows=4), -1.0, ALU.mult)
        P.act(NBF[l][0:8, 1:2], cvc(l, "a_log", rows=8), AF.Exp)

    total_blocks = n_tiles * depth * len(WSEQ)
    wstate = {"issued": 0, "used": 0}

    def _issue(k):
        tl, bi = divmod(k, len(WSEQ))
        l = tl % depth
        o, kc, ncols = WB[WSEQ[bi]]
        slot = RING[k % NRING]
        P.dma(slot[:, 0:kc * ncols], wbig_d[l][:, o:o + kc * ncols], queue="pool")

    def wget(l, name):
        k = wstate["used"]
        tl, bi = divmod(k, len(WSEQ))
        assert WSEQ[bi] == name and tl % depth == l, (WSEQ[bi], name, tl, l)
        while wstate["issued"] < min(total_blocks, k + NRING):
            _issue(wstate["issued"])
            wstate["issued"] += 1
        wstate["used"] += 1
        o, kc, ncols = WB[name]
        return RING[k % NRING], kc, ncols

    def dump(name, ti, l, views):
        if name in dbg and dbg[name][0] == ti and dbg[name][1] == l:
            for i, v in enumerate(views):
                n = v.shape[1]
                final_evs.append(P.dma(dbg_d[name][0:v.shape[0], i * n:(i + 1) * n], v))

    def rmsnorm_stats(src, nchunks, dim, rs_out):
        ps = ps_dense()
        for c in range(nchunks):
            sq = SQ[c % 2]
            if c % 2 == 0:
                P.act(sq.r(), src[c], AF.Square)
            else:
                P.tt(sq.r(), src[c], src[c], ALU.mult)
            P.mm(ps, ONESR, sq, start=(c == 0), stop=(c == nchunks - 1))
        P.act(rs_out, ps, AF.Ln, bias=EPS, scale=1.0 / dim)
        P.act(rs_out, rs_out, AF.Exp, scale=-0.5)

    def project(l, name, rhs_list, out_ps, M=128, lcol0=0, first=True, last=True, blk=None):
        slot, kc, ncols = blk if blk is not None else wget(l, name)
        for c in range(kc):
            P.mm(out_ps, slot[:, c * ncols + lcol0: c * ncols + lcol0 + M], rhs_list[c],
                 start=(first and c == 0), stop=(last and c == kc - 1))
        return slot, kc, ncols

    def conv4(l, acc, out_final, src_slot, wname, bname, c, nch, halo_v, src_r):
        P.copy(src_slot[:, 0:HALO].r() if src_r else src_slot[:, 0:HALO], halo_v)
        w = lambda k: CVt[l][:, CV[wname] + k * nch + c: CV[wname] + k * nch + c + 1]
        b = CVt[l][:, CV[bname] + c: CV[bname] + c + 1]
        P.ts(acc, src_slot[:, 3:3 + T], w(3), ALU.mult, b, ALU.add)
        for k in (2, 1):
            P.stt(acc, src_slot[:, k:k + T], w(k), acc, ALU.mult, ALU.add)
        P.stt(out_final, src_slot[:, 0:T], w(0), acc, ALU.mult, ALU.add)
        P.copy(halo_v.r() if src_r else halo_v, src_slot[:, T:T + 3])

    for ti in range(n_tiles):
        t0 = ti * T
        for c in range(8):
            P.dma(X[c], xT[c * 128:(c + 1) * 128, t0:t0 + T])
        for l in range(depth):
            P.dma(WSt[:, 0:1280], wsm_d[l][:, 0:1280], queue="pool")
            P.dma(WSt[:, 1280:2560], wsm_d[l][:, 1280:2560], queue="pool")
            RSt = TMP[2]
            dense_banks[0] = [0, 1, 4, 5, 6, 7]
            rmsnorm_stats(X, 8, D, RSt)
            for c in range(8):
                P.stt(H[c].r(), X[c], cvc(l, "g_mix_pre", c), RSt, ALU.mult, ALU.mult)
            dump("h", ti, l, H)
            if stop == "h":
                break
            YM = RS_[0:4]
            YR = RS_[4:8]
            YS = RS_[8:12]
            W_ = RS_[12:28]

            XM, XC, QT, KT = W_[0:4], W_[4:8], W_[8:12], W_[12:16]
            for i in range(4):
                ps = ps_dense()
                project(l, "xm%d" % i, H, ps)
                P.copy(XM[i][:, 3:3 + T].r(), ps, eng="act")
            if stop == "xm":
                break
            gblk = wget(l, "gates")
            IMr, Cr, Ar, EBr, DTr = ROW[0], ROW[1], ROW[2], ROW[3], ROW[4]
            ps = ps_dense()
            project(l, None, H, ps[0:4, :], M=4, lcol0=0, blk=gblk)
            P.copy(IMr[0:4, :], ps[0:4, :], eng="act")
            ps = ps_dense()
            project(l, None, H, ps[0:4, :], M=4, lcol0=4, blk=gblk)
            P.act(Cr[0:4, :], ps[0:4, :], AF.Exp, bias=NBF[l][0:4, 0:1], scale=-1.0)
            P.act(Cr[0:4, :], Cr[0:4, :], AF.Ln, bias=1.0)
            ps = ps_dense()
            project(l, None, H, ps[0:8, :], M=8, lcol0=8, blk=gblk)
            P.act(DTr[0:8, :], ps[0:8, :], AF.Exp, bias=cvc(l, "dt_bias", rows=8))
            P.act(DTr[0:8, :], DTr[0:8, :], AF.Ln, bias=1.0)
            P.scan(Cr[0:4, :], RESET[0:4, :], Cr[0:4, :], 0.0, ALU.mult, ALU.add)
            P.tt(Ar[0:4, :], IMr[0:4, :], Cr[0:4, :], ALU.add)
            P.ts(Ar[0:4, :], Ar[0:4, :], cvc(l, "b_i", rows=4), ALU.add, -0.5 * float(np.log(128.0)), ALU.add)
            P.act(Ar[0:4, :], Ar[0:4, :], AF.Exp)
            P.act(EBr[0:4, :], Cr[0:4, :], AF.Exp, scale=-1.0)
            dump("mrows", ti, l, [Cr[0:4, :], Ar[0:4, :], EBr[0:4, :], DTr[0:8, :]])
            for c in range(NCH):
                pst = ps_small()
                P.transpose(pst[0:64, 0:4], Ar[0:4, c * LC:(c + 1) * LC], IDENT[0:4, 0:4])
                P.copy(GT[:, c * 32: c * 32 + 4], pst[0:64, 0:4])
            if stop == "gates":
                break
            cacc = FS[2][:, 0:T]
            for i in range(4):
                conv4(l, cacc, cacc, XM[i], "cmw", "cmb", i, 4, HMR[l][:, i * 3:i * 3 + 3], True)
                P.act(XC[i][:, 0:T].r(), cacc, AF.Silu)
            dump("xc", ti, l, [v[:, 0:T] for v in XC])
            if stop == "conv":
                break
            for h in range(4):
                ps = ps_dense()
                P.mm(ps, WSt[:, WS_Q + h * 128: WS_Q + (h + 1) * 128], XC[h][:, 0:T])
                P.copy(QT[h][:, 0:T].r(), ps, eng="act")
                ps = ps_dense()
                P.mm(ps, WSt[:, WS_K + h * 128: WS_K + (h + 1) * 128], XC[h][:, 0:T])
                P.copy(KT[h][:, 0:T].r(), ps, eng="act")
            if stop == "qk":
                break
            EBT = [[TMP[0], TMP[1]], [FS[7][:, 0:T], FS[8][:, 0:T]]]
            post_q = []
            dense_banks[0] = [0, 1]
            for hp in range(2):
                heads = (2 * hp, 2 * hp + 1)
                NUM = {heads[0]: PL[0], heads[1]: PL[1]}
                DEN = {heads[0]: PL[2], heads[1]: PL[3]}
                EBS = {heads[0]: EBT[hp][0], heads[1]: EBT[hp][1]}
                for h in heads:
                    ps = ps_dense()
                    bcast_row(ps, EBr, h, 4, ROW[0])
                    P.copy(EBS[h], ps, eng="act")
                if stop == "ebs":
                    break
                for c in range(NCH):
                    cs = slice(c * LC, (c + 1) * LC)
                    if stop in ("c0", "c0a", "c0a1", "c0b", "c0c") and c == 1:
                        break
                    pkk = ps_dense()
                    pkv = ps_dense()
                    for q, h in enumerate(heads):
                        P.mm(pkk[0:64, q * 128:(q + 1) * 128], XC[h][:, cs], WSt[:, WS_K + h * 128: WS_K + (h + 1) * 128])
                    for q, h in enumerate(heads):
                        P.mm(pkv[0:64, q * 128:(q + 1) * 128], XM[h][:, 3 + c * LC: 3 + (c + 1) * LC],
                             WSt[:, WS_V + h * 128: WS_V + (h + 1) * 128])
                    psts = []
                    for q, h in enumerate(heads):
                        pst = ps_small()
                        P.mm(pst[0:64, 0:64], KT[h][:, cs], QT[h][:, cs])
                        psts.append(pst)
                    KA, VE = TK[(c % 2) * 2], TK[(c % 2) * 2 + 1]
                    P.tt(KA.rr("p (h d) -> p h d", h=2).r(), pkk[0:64, 0:256].rr("p (h d) -> p h d", h=2),
                         GT[:, c * 32 + heads[0]: c * 32 + heads[0] + 2].us(2).bc([64, 2, 128]), ALU.mult)
                    P.copy(VE.r(), pkv[0:64, 0:256], eng="act")
                    ats = []
                    for q, h in enumerate(heads):
                        at = AT_[(2 * c + q) % 4]
                        P.stt(at.r(), psts[q][0:64, 0:64], GT[:, c * 32 + h: c * 32 + h + 1], MASKT, ALU.mult, ALU.mult)
                        ats.append(at)
                    for q, h in enumerate(heads):
                        at = ats[q]
                        P.mm(NUM[h][:, cs], VE[:, q * 128:(q + 1) * 128], at, start=True, stop=False)
                        P.mm(NUM[h][:, cs], CS[l][h][:, 0:128], QT[h][:, cs], start=False, stop=True)
                        P.mm(DEN[h][:, cs], ONESR[0:64, :], at, start=True, stop=False)
                        P.mm(DEN[h][:, cs], CS[l][h][:, 128:256], QT[h][:, cs], start=False, stop=True)
                    for q, h in enumerate(heads):
                        pu = ps_small()
                        P.mm(pu[:, 0:128], KA[:, q * 128:(q + 1) * 128], VE[:, q * 128:(q + 1) * 128])
                        P.mm(pu[:, 128:256], KA[:, q * 128:(q + 1) * 128], ONESR[0:64, :])
                        P.tt(CSTMP[q], CS[l][h], pu[:, 0:256], ALU.add)
                        P.ts(CS[l][h].r(), CSTMP[q], EBS[h][:, c * LC + LC - 1: c * LC + LC], ALU.mult)
                    for _ in range(4):
                        if post_q:
                            post_q.pop(0)()
                while post_q:
                    post_q.pop(0)()
                if stop in ("c0", "cloop", "c0a", "c0a1", "c0b", "c0c"):
                    break
                for q, h in enumerate(heads):
                    if hp == 0:
                        nsrc, dsrc = FS[3 + 2 * q][:, 0:T], FS[4 + 2 * q][:, 0:T]
                        P.copy(nsrc, NUM[h], eng="act")
                        P.copy(dsrc, DEN[h], eng="act")
                    else:
                        nsrc, dsrc = NUM[h], DEN[h]

                    def mk(h=h, nsrc=nsrc, dsrc=dsrc, ebs=EBS[h]):
                        dn, t2, sg = TMP[2], FS[0][:, 0:T], FS[1][:, 0:T]
                        st = {}

                        def f_proj():
                            st["ps"] = ps_dense()
                            project(l, "om%d" % h, H, st["ps"])
                            P.act(sg, st["ps"], AF.Sigmoid)

                        def f_norm_mm():
                            P.act(SQ[0].r(), t2, AF.Square)
                            ps = ps_dense()
                            P.mm(ps, ONESR, SQ[0])
                            P.act(dn, ps, AF.Ln, bias=EPS, scale=1.0 / 128)
                        return [
                            f_proj,
                            lambda: P.tt(dn, dsrc, ebs, ALU.mult),
                            lambda: P.ts(t2, dn, -1.0, ALU.mult, 1.0, ALU.max),
                            lambda: P.tt(t2, t2, dn, ALU.max),
                            lambda: P.act(t2, t2, AF.Ln),
                            lambda: P.act(t2, t2, AF.Exp, scale=-1.0),
                            lambda: P.tt(t2, t2, ebs, ALU.mult),
                            lambda: P.tt(t2, nsrc, t2, ALU.mult),
                            lambda: P.tt(t2, t2, sg, ALU.mult),
                            f_norm_mm,
                            lambda: P.act(dn, dn, AF.Exp, scale=-0.5),
                            lambda: P.stt(YM[h][:, 0:T].r(), t2, cvc(l, "norm_m", h), dn, ALU.mult, ALU.mult),
                        ]
                    post_q.extend(mk())
            while post_q:
                post_q.pop(0)()
            dense_banks[0] = [0, 1, 4, 5, 6, 7]
            dump("ym", ti, l, [y[:, 0:T] for y in YM])
            if stop in ("mlstm", "ebs", "c0", "cloop", "c0a", "c0a1", "c0b", "c0c"):
                break

            XR = FS[0:4]
            RT = [[FS[4 + 4 * b + k][:, 0:T] for k in range(4)] for b in range(2)]
            cacc = TMP[0]
            XCR = W_[0:4]
            for i in range(4):
                ps = ps_dense()
                project(l, "xr%d" % i, H, ps)
                P.copy(XR[i][:, 3:3 + T], ps, eng="act")

            def rg_a(j):
                Rt, It, At, St = RT[j % 2]
                conv4(l, cacc, XCR[j][:, 0:T].r(), XR[j], "crw", "crb", j, 4, HM[l][:, j * 3:j * 3 + 3], False)
                ps = ps_dense()
                P.mm(ps, WSt[:, WS_A + j * 128: WS_A + (j + 1) * 128], XCR[j][:, 0:T])
                P.act(Rt, ps, AF.Sigmoid, bias=cvc(l, "b_a", j))
                ps = ps_dense()
                P.mm(ps, WSt[:, WS_X + j * 128: WS_X + (j + 1) * 128], XCR[j][:, 0:T])
                P.act(It, ps, AF.Sigmoid, bias=cvc(l, "b_x", j))
                P.act(At, Rt, AF.Exp, scale=NCF[l][:, j:j + 1])
                P.act(St, At, AF.Square)
                P.act(St, St, AF.Sqrt, bias=1.0, scale=-1.0)

            def rg_b(j):
                Rt, It, At, St = RT[j % 2]
                P.tt(It, It, XCR[j][:, 0:T], ALU.mult)
                P.tt(It, It, St, ALU.mult)
                P.scan(Rt, At, It, HST[l][:, j:j + 1], ALU.mult, ALU.add)
                P.copy(HST[l][:, j:j + 1], Rt[:, T - 1:T])
                ps = ps_dense()
                project(l, "yr%d" % j, H, ps)
                P.act(St, ps, AF.Gelu_apprx_tanh)
                P.tt(YR[j][:, 0:T].r(), Rt, St, ALU.mult)
            rg_a(0)
            for j in range(4):
                if j + 1 < 4:
                    rg_a(j + 1)
                rg_b(j)
            dump("yr", ti, l, [y[:, 0:T] for y in YR])
            if stop == "rglru":
                break

            XBC = FS[0:8]
            cacc = FS[8][:, 0:T]
            XSC = W_[0:8]
            for i in range(8):
                ps = ps_dense()
                project(l, "xbc%d" % i, H, ps)
                P.copy(XBC[i][:, 3:3 + T], ps, eng="act")
            for i in range(8):
                conv4(l, cacc, cacc, XBC[i], "csw", "csb", i, 8, HM[l][:, 12 + i * 3: 12 + i * 3 + 3], False)
                P.act(XSC[i][:, 0:T].r(), cacc, AF.Silu)
            XS, BT, CT = [v[:, 0:T] for v in XSC[0:4]], [v[:, 0:T] for v in XSC[4:6]], [v[:, 0:T] for v in XSC[6:8]]
            ACN, Wr = ROW[0], ROW[1]
            P.ts(ACN[0:8, :], DTr[0:8, :], NBF[l][0:8, 1:2], ALU.mult)
            P.scan(ACN[0:8, :], RESET[0:8, :], ACN[0:8, :], 0.0, ALU.mult, ALU.add)
            P.tt(Wr[0:8, :].rr("p (c t) -> p c t", t=LC),
                 ACN[0:8, :].rr("p (c t) -> p c t", t=LC)[:, :, LC - 1:LC].bc([8, NCH, LC]),
                 ACN[0:8, :].rr("p (c t) -> p c t", t=LC), ALU.subtract)
            P.act(Wr[0:8, :], Wr[0:8, :], AF.Exp, scale=-1.0)
            P.tt(Wr[0:8, :], Wr[0:8, :], DTr[0:8, :], ALU.mult)
            dump("srows", ti, l, [ACN[0:8, :], Wr[0:8, :], DTr[0:8, :]])
            for c in range(NCH):
                pst = ps_small()
                cs = slice(c * LC, (c + 1) * LC)
                P.transpose(pst[0:64, 0:8], ACN[0:8, cs], IDENT[0:8, 0:8])
                P.transpose(pst[0:64, 8:16], DTr[0:8, cs], IDENT[0:8, 0:8])
                P.transpose(pst[0:64, 16:24], Wr[0:8, cs], IDENT[0:8, 0:8])
                P.copy(GT[:, c * 32 + 4: c * 32 + 28], pst[0:64, 0:24])
            GT3 = GT.rr("p (c k) -> p c k", k=32)
            GY = [FS[i][:, 0:T] for i in range(4)]
            dense_banks[0] = [0, 1]
            for g in range(2):
                pcb = ps_dense()
                for c in range(NCH):
                    cs = slice(c * LC, (c + 1) * LC)
                    P.mm(pcb[0:64, cs], BT[g][:, cs], CT[g][:, cs])
                P.tt(CBM.rr("p (c t) -> p c t", t=LC), pcb[0:64, :].rr("p (c t) -> p c t", t=LC),
                     MASKT.us(1).bc([64, NCH, LC]), ALU.mult)
                EAC2 = [TMP[0], TMP[1]]
                EATOT = SMALL
                for q in range(4):
                    h = 4 * g + q
                    pa = ps_dense()
                    bcast_row(pa, ACN, h, 8, ROW[2])
                    half = slice(64 * (q % 2), 64 * (q % 2) + 64)
                    P.act(EAC2[q // 2][half, :], pa[half, :], AF.Exp, scale=-1.0)
                    P.act(EATOT[:, q * 8:(q + 1) * 8], pa.rr("p (c t) -> p c t", t=LC)[:, :, LC - 1], AF.Exp, scale=-1.0)
                    P.tt(SEGT.rr("p (c t) -> p c t", t=LC), GT3[:, :, 4 + h:5 + h].bc([64, NCH, LC]),
                         pa[0:64, :].rr("p (c t) -> p c t", t=LC), ALU.subtract)
                    P.ts(SEGT, SEGT, 0.0, ALU.min)
                    P.act(SEGT, SEGT, AF.Exp)
                    P.tt(SEGT.rr("p (c t) -> p c t", t=LC), SEGT.rr("p (c t) -> p c t", t=LC),
                         GT3[:, :, 12 + h:13 + h].bc([64, NCH, LC]), ALU.mult)
                    P.tt(MT[q].r(), SEGT, CBM, ALU.mult)
                YD = [PL[0], PL[1]]
                YO = [PL[2], PL[3]]
                for c in range(NCH):
                    cs = slice(c * LC, (c + 1) * LC)
                    pxt = ps_small()
                    for jj in range(2):
                        P.transpose(pxt[0:64, jj * 128:(jj + 1) * 128], XS[2 * g + jj][:, cs], IDENT)
                    P.transpose(pxt[0:64, 256:384], BT[g][:, cs], IDENT)
                    XT_, XW, BTK = TK[(c % 2) * 3], TK[(c % 2) * 3 + 1], TK[(c % 2) * 3 + 2]
                    P.copy(XT_.r(), pxt[0:64, 0:256], eng="act")
                    P.tt(XW.rr("p (h d) -> p h d", h=4).r(), pxt[0:64, 0:256].rr("p (h d) -> p h d", h=4),
                         GT[:, c * 32 + 20 + 4 * g: c * 32 + 24 + 4 * g].us(2).bc([64, 4, 64]), ALU.mult)
                    P.copy(BTK[:, 0:128].r(), pxt[0:64, 256:384], eng="act")
                    for jj in range(2):
                        j = 2 * g + jj
                        for q2 in range(2):
                            q = 2 * jj + q2
                            P.mm(YD[jj][64 * q2:64 * q2 + 64, cs], XT_[:, q * 64:(q + 1) * 64], MT[q][:, cs], r32=False)
                        P.mm(YO[jj][:, cs], SST[l][j], CT[g][:, cs])
                    pu = ps_small()
                    P.mm(pu[:, 0:256], BTK[:, 0:128], XW)
                    for jj in range(2):
                        j = 2 * g + jj
                        for q2 in range(2):
                            q = 2 * jj + q2
                            P.stt(SST[l][j][:, q2 * 64:(q2 + 1) * 64].r(), SST[l][j][:, q2 * 64:(q2 + 1) * 64],
                                  EATOT[:, q * 8 + c: q * 8 + c + 1], pu[:, q * 64:(q + 1) * 64], ALU.mult, ALU.add)
                for jj in range(2):
                    j = 2 * g + jj
                    y, sz = TMP[2], FS[9][:, 0:T]
                    ps = ps_dense()
                    project(l, "zs%d" % j, H, ps)
                    P.act(sz, ps, AF.Silu)
                    P.tt(y, YO[jj], EAC2[jj], ALU.mult)
                    P.tt(y, y, YD[jj], ALU.add)
                    P.stt(y, XS[j], cvc(l, "dskip", j), y, ALU.mult, ALU.add)
                    P.tt(GY[j], y, sz, ALU.mult)
            dense_banks[0] = [0, 1, 2, 3, 4, 5, 6, 7]
            rs = TMP[2]
            rmsnorm_stats(GY, 4, 512, rs)
            for j in range(4):
                P.stt(YS[j][:, 0:T].r(), GY[j], cvc(l, "norm_s", j), rs, ALU.mult, ALU.mult)
            dump("ys", ti, l, [y[:, 0:T] for y in YS])
            if stop == "ssd":
                break

            YALL = [y[:, 0:T] for y in (YM + YR + YS)]
            MIX = [s[:, 0:T] for s in FS[0:8]]
            for n in range(8):
                ps = ps_dense()
                project(l, "out%da" % n, YALL[0:8], ps, last=False)
                project(l, "out%db" % n, YALL[8:12], ps, first=False)
                P.copy(MIX[n], ps, eng="act")
            rs = TMP[2]
            rmsnorm_stats(MIX, 8, D, rs)
            for c in range(8):
                P.stt(MIX[c], MIX[c], cvc(l, "g_mix_post", c), rs, ALU.mult, ALU.mult)
                P.tt(X[c], X[c], MIX[c], ALU.add)
            dump("x1", ti, l, X)
            if stop == "mix":
                break

            rs = TMP[2]
            rmsnorm_stats(X, 8, D, rs)
            for c in range(8):
                P.stt(H[c].r(), X[c], cvc(l, "g_ffn_pre", c), rs, ALU.mult, ALU.mult)
            ACTF = [s[:, 0:T] for s in RS_[0:NJ]]
            UB = FS[0:4]
            DST = [[TMP[0], TMP[1]], [FS[4][:, 0:T], FS[5][:, 0:T]]]
            dense_banks[0] = [0, 1, 2, 3, 4, 5, 6, 7]
            for j in range(NJ + 1):
                if j < NJ:
                    for q, nm in enumerate(("upg", "upv")):
                        ps = ps_dense()
                        project(l, "%s%d" % (nm, j), H, ps)
                        ub = UB[(j % 2) * 2 + q]
                        cidx = q * NJ + j
                        w = lambda k: CVt[l][:, CV["cfw"] + k * 44 + cidx: CV["cfw"] + k * 44 + cidx + 1]
                        b = CVt[l][:, CV["cfb"] + cidx: CV["cfb"] + cidx + 1]
                        dst = DST[j % 2][q]
                        P.act(dst, ps, AF.Identity, bias=b, scale=w(2))
                        P.copy(ub[:, 3:3 + T], ps, eng="act")
                        P.copy(ub[:, 1:3], HF[l][:, cidx * 2: cidx * 2 + 2], eng=HALO_ENG)
                        P.stt(dst, ub[:, 2:2 + T], w(1), dst, ALU.mult, ALU.add)
                        P.stt(dst, ub[:, 1:1 + T], w(0), dst, ALU.mult, ALU.add)
                        P.copy(HF[l][:, cidx * 2: cidx * 2 + 2], ub[:, T + 1:T + 3], eng=HALO_ENG)
                if j > 0:
                    rg, rv = DST[(j - 1) % 2]
                    P.act(rg, rg, AF.Gelu_apprx_tanh)
                    P.tt(ACTF[j - 1].r(), rg, rv, ALU.mult)
            FO = [s[:, 0:T] for s in FS[4:12]]
            for n in range(8):
                ps = ps_dense()
                project(l, "dn%da" % n, ACTF[0:8], ps, last=False)
                project(l, "dn%db" % n, ACTF[8:16], ps, first=False, last=False)
                project(l, "dn%dc" % n, ACTF[16:22], ps, first=False)
                P.copy(FO[n], ps, eng="act")
            rs = TMP[2]
            rmsnorm_stats(FO, 8, D, rs)
            for c in range(8):
                P.stt(FO[c], FO[c], cvc(l, "g_ffn_post", c), rs, ALU.mult, ALU.mult)
                P.tt(X[c], X[c], FO[c], ALU.add)
            dump("x2", ti, l, X)
        else:
            for c in range(8):
                final_evs.append(P.dma(outT[c * 128:(c + 1) * 128, t0:t0 + T], X[c]))
            continue
        break
    P.emit(final_events=[e for e in final_evs if e is not None])
    return nc, P


_CACHE = {}


def kernel(**inputs):
    x = np.asarray(inputs["x"], np.float32)
    B = x.shape[0]
    wbig, wsm, cvec = host_layout(inputs)
    cmat = host_consts()
    if "nc" not in _CACHE:
        _CACHE["nc"] = build()[0]
    nc = _CACHE["nc"]
    in_maps = []
    for b in range(B):
        m = {"xT": np.ascontiguousarray(x[b].T), "cvec": cvec, "cmat": cmat}
        for l in range(DEPTH):
            m["wbig%d" % l] = wbig[l]
            m["wsm%d" % l] = wsm[l]
        in_maps.append(m)
    res = run_bass_kernel_spmd(nc, in_maps, core_ids=list(range(B)))
    out = np.stack([np.ascontiguousarray(r["outT"].T) for r in res.results], axis=0)
    return out.astype(np.float32)
```
